# Optimizing a Trainium2 kernel written in Bass

```python
import jax, jax.numpy as jnp
from jax import lax
import numpy as np

D_MODEL = 2048
BATCH = 2
SEQ = 4096
DEPTH = 1

CONV_DIM = 1024
CONV_GROUPS = 8
CONV_WIDTH = 3
N_HEADS = 16
QK_NOPE_DIM = 128
QK_ROPE_DIM = 64
QK_HEAD_DIM = QK_NOPE_DIM + QK_ROPE_DIM
V_HEAD_DIM = 128
Q_LORA_RANK = 768
KV_LORA_RANK = 512
ROPE_THETA = 10000.0
Q_BLOCK = 128
D_FF = 5632
NORM_EPS = 1e-6
NEG_INF = -1e30

IN_SIZES = (CONV_DIM, CONV_DIM, CONV_DIM, Q_LORA_RANK, KV_LORA_RANK, QK_ROPE_DIM, 2 * D_MODEL)
IN_WIDTH = sum(IN_SIZES)
IN_SPLIT_IDX = tuple(int(i) for i in np.cumsum(IN_SIZES)[:-1])

kernel_name = "hybrid_gated_conv_mla_convffn_block"


def rmsnorm(x, g):
    xf = x.astype(jnp.float32)
    r = lax.rsqrt(jnp.mean(xf * xf, axis=-1, keepdims=True) + NORM_EPS)
    return (xf * r).astype(x.dtype) * g


def causal_dwconv3(u, w):
    s = u.shape[1]
    up = jnp.pad(u, ((0, 0), (CONV_WIDTH - 1, 0), (0, 0)))
    y = w[0] * up[:, 0:s]
    for k in range(1, CONV_WIDTH):
        y = y + w[k] * up[:, k:k + s]
    return y


def rope_tables(positions, dtype):
    inv_freq = ROPE_THETA ** (-jnp.arange(0, QK_ROPE_DIM, 2, dtype=jnp.float32) / QK_ROPE_DIM)
    ang = positions.astype(jnp.float32)[..., None] * inv_freq
    cos = jnp.cos(ang)[:, :, None, :].astype(dtype)
    sin = jnp.sin(ang)[:, :, None, :].astype(dtype)
    return cos, sin


def apply_rope_tail(x, cos, sin):
    x_nope = x[..., :QK_NOPE_DIM]
    x_r = x[..., QK_NOPE_DIM:]
    x1, x2 = jnp.split(x_r, 2, axis=-1)
    rot = jnp.concatenate([x1 * cos - x2 * sin, x2 * cos + x1 * sin], axis=-1)
    return jnp.concatenate([x_nope, rot], axis=-1)


def causal_block_attention(q, k, v):
    b, s, h, dk = q.shape
    dv = v.shape[-1]
    nb = s // Q_BLOCK
    scale = dk ** -0.5
    qb = q.reshape(b, nb, Q_BLOCK, h, dk).transpose(1, 0, 3, 2, 4)
    kh = k.transpose(0, 2, 1, 3)
    vh = v.transpose(0, 2, 1, 3)
    kpos = jnp.arange(s)

    def one_block(args):
        q_blk, blk = args
        sc = jnp.einsum('bhqd,bhkd->bhqk', q_blk, kh).astype(jnp.float32) * scale
        qpos = blk * Q_BLOCK + jnp.arange(Q_BLOCK)
        mask = kpos[None, :] <= qpos[:, None]
        sc = jnp.where(mask, sc, NEG_INF)
        p = jax.nn.softmax(sc, axis=-1).astype(vh.dtype)
        return jnp.einsum('bhqk,bhkd->bhqd', p, vh)

    o = lax.map(one_block, (qb, jnp.arange(nb)))
    return o.transpose(1, 0, 3, 2, 4).reshape(b, s, h * dv)


def setup_inputs(seed: int = 0) -> dict:
    key = jax.random.key(seed)
    ks = jax.random.split(key, 20)

    def dense(k, fan_in, fan_out):
        return jax.random.normal(k, (DEPTH, fan_in, fan_out), jnp.float32) * fan_in ** -0.5

    def gain(k, n):
        return 1.0 + 0.02 * jax.random.normal(k, (DEPTH, n), jnp.float32)

    x = jax.random.normal(ks[0], (BATCH, SEQ, D_MODEL), jnp.float32)
    positions = jnp.broadcast_to(jnp.arange(SEQ, dtype=jnp.int32), (BATCH, SEQ))
    return {
        "x": x,
        "positions": positions,
        "ln1_g": gain(ks[1], D_MODEL),
        "w_in": dense(ks[2], D_MODEL, IN_WIDTH),
        "b_gate": 0.01 * jax.random.normal(ks[3], (DEPTH, 2 * D_MODEL), jnp.float32),
        "conv_w": jax.random.normal(ks[4], (DEPTH, CONV_WIDTH, CONV_DIM), jnp.float32) * CONV_WIDTH ** -0.5,
        "w_conv_out": dense(ks[5], CONV_DIM, D_MODEL),
        "q_a_g": gain(ks[6], Q_LORA_RANK),
        "w_q_b": dense(ks[7], Q_LORA_RANK, N_HEADS * QK_HEAD_DIM),
        "kv_a_g": gain(ks[8], KV_LORA_RANK),
        "w_kv_b": dense(ks[9], KV_LORA_RANK, N_HEADS * (QK_NOPE_DIM + V_HEAD_DIM)),
        "q_norm_g": gain(ks[10], QK_HEAD_DIM),
        "k_norm_g": gain(ks[11], QK_HEAD_DIM),
        "w_mla_out": dense(ks[12], N_HEADS * V_HEAD_DIM, D_MODEL),
        "w_o": dense(ks[13], D_MODEL, D_MODEL),
        "ln2_g": gain(ks[14], D_MODEL),
        "w_ffn_up": dense(ks[15], D_MODEL, 2 * D_FF),
        "ffn_conv_w": jax.random.normal(ks[16], (DEPTH, CONV_WIDTH, 2 * D_FF), jnp.float32) * CONV_WIDTH ** -0.5,
        "ffn_conv_b": 0.01 * jax.random.normal(ks[17], (DEPTH, 2 * D_FF), jnp.float32),
        "w_ffn_down": dense(ks[18], D_FF, D_MODEL),
    }


def reference(x, positions, ln1_g, w_in, b_gate, conv_w, w_conv_out, q_a_g, w_q_b, kv_a_g, w_kv_b,
              q_norm_g, k_norm_g, w_mla_out, w_o, ln2_g, w_ffn_up, ffn_conv_w, ffn_conv_b, w_ffn_down):
    b, s, _ = x.shape
    cos, sin = rope_tables(positions, x.dtype)
    h = x
    for l in range(DEPTH):
        u = rmsnorm(h, ln1_g[l])
        z = u @ w_in[l]
        zb, zc, zv, q_lat, kv_lat, k_rope, gates = jnp.split(z, IN_SPLIT_IDX, axis=-1)
        gates = jax.nn.sigmoid(gates + b_gate[l])
        gate_a, gate_b = jnp.split(gates, 2, axis=-1)

        y_conv = (zb * causal_dwconv3(zc * zv, conv_w[l])) @ w_conv_out[l]

        q = (rmsnorm(q_lat, q_a_g[l]) @ w_q_b[l]).reshape(b, s, N_HEADS, QK_HEAD_DIM)
        kv = (rmsnorm(kv_lat, kv_a_g[l]) @ w_kv_b[l]).reshape(b, s, N_HEADS, QK_NOPE_DIM + V_HEAD_DIM)
        k_nope, v = jnp.split(kv, [QK_NOPE_DIM], axis=-1)
        k = jnp.concatenate(
            [k_nope, jnp.broadcast_to(k_rope[:, :, None, :], (b, s, N_HEADS, QK_ROPE_DIM))], axis=-1)
        q = apply_rope_tail(rmsnorm(q, q_norm_g[l]), cos, sin)
        k = apply_rope_tail(rmsnorm(k, k_norm_g[l]), cos, sin)
        y_mla = causal_block_attention(q, k, v) @ w_mla_out[l]

        h = h + (gate_a * y_conv + gate_b * y_mla) @ w_o[l]

        u = rmsnorm(h, ln2_g[l])
        a = causal_dwconv3(u @ w_ffn_up[l], ffn_conv_w[l]) + ffn_conv_b[l]
        a_gate, a_up = jnp.split(a, 2, axis=-1)
        h = h + (jax.nn.silu(a_gate) * a_up) @ w_ffn_down[l]
    return h
```

```python
import math
import numpy as np
import concourse.bass as bass
import concourse.mybir as mybir
from concourse.bass_utils import run_bass_kernel_spmd

F32 = mybir.dt.float32
BF16 = mybir.dt.bfloat16
I32 = mybir.dt.int32
U8 = mybir.dt.uint8
AF = mybir.ActivationFunctionType
ALU = mybir.AluOpType

PE, ACT, DVE, POOL, SP = "tensor", "scalar", "vector", "gpsimd", "sync"
STRICT_SAME_ENGINE = True
ENGS = (PE, ACT, DVE, POOL, SP)


class T:
    __slots__ = ("name", "w", "r", "excl")

    def __init__(self, name="", excl=False):
        self.name = name
        self.w = None
        self.r = []
        self.excl = excl


class Op:
    __slots__ = ("eng", "fn", "deps", "is_dma", "chan", "done_val", "needs_inc", "waits")

    def __init__(self, eng, fn, is_dma, chan):
        self.eng = eng
        self.fn = fn
        self.is_dma = is_dma
        self.chan = chan
        self.deps = []
        self.done_val = None
        self.needs_inc = False
        self.waits = None


class Sched:
    def __init__(self, nc):
        self.nc = nc
        self.ops = {e: [] for e in ENGS}
        self.chan_cnt = {}
        self.chan_sem = {}
        self.eng_sem = {}
        self.last = {e: None for e in ENGS}

    def add(self, eng, fn, reads=(), writes=(), chan=None, extra_deps=()):
        is_dma = chan is not None
        o = Op(eng, fn, is_dma, chan)
        deps = {}

        def consider(d, kind):
            if d is None or d is o:
                return
            if d.is_dma or is_dma or d.eng != eng:
                deps[id(d)] = d
            elif eng == PE:
                return
            elif kind == "RAW" or STRICT_SAME_ENGINE:
                deps[id(d)] = d

        for t in reads:
            consider(t.w, "RAW")
            if t.excl:
                for r in t.r:
                    if r.eng != eng:
                        deps[id(r)] = r
        for t in writes:
            consider(t.w, "WAW")
            for r in t.r:
                consider(r, "WAR")
        for d in extra_deps:
            if d is not None and d is not o:
                deps[id(d)] = d
        for t in reads:
            t.r.append(o)
        for t in writes:
            t.w = o
            t.r = []
        o.deps = list(deps.values())
        if is_dma:
            c = self.chan_cnt.get(chan, 0) + 1
            self.chan_cnt[chan] = c
            o.done_val = 16 * c
        self.ops[eng].append(o)
        if not is_dma:
            self.last[eng] = o
        return o

    def barrier(self, engines=(PE, ACT, DVE, SP)):
        lasts = {e: self.last[e] for e in (PE, ACT, DVE)}
        for e in engines:
            deps = [lasts[x] for x in lasts if x != e and lasts[x] is not None]
            self.add(e, lambda eng: eng.nop(), extra_deps=deps)
            if e in lasts:
                self.last[e] = lasts[e]

    def finalize(self):
        nc = self.nc
        pos = {}
        for e in ENGS:
            for i, o in enumerate(self.ops[e]):
                pos[id(o)] = i
        for e in ENGS:
            waited = {}
            for o in self.ops[e]:
                best = {}
                for d in o.deps:
                    key = ("c", d.chan) if d.is_dma else ("e", d.eng)
                    p = d.done_val if d.is_dma else pos[id(d)]
                    if waited.get(key, -1) >= p:
                        continue
                    if key not in best or best[key][0] < p:
                        best[key] = (p, d)
                o.deps = [d for (_, d) in best.values()]
                for key, (p, d) in best.items():
                    waited[key] = p
                    d.needs_inc = True
        for e in ENGS:
            self.eng_sem[e] = nc.alloc_semaphore("sem_" + e)
            c = 0
            for o in self.ops[e]:
                if not o.is_dma and o.needs_inc:
                    c += 1
                    o.done_val = c
        for ch in self.chan_cnt:
            self.chan_sem[ch] = nc.alloc_semaphore("dsem_" + str(ch))
        ninc = 0
        nwait = 0
        for e in ENGS:
            for o in self.ops[e]:
                ws = []
                for d in o.deps:
                    sem = self.chan_sem[d.chan] if d.is_dma else self.eng_sem[d.eng]
                    ws.append((sem, d.done_val))
                o.waits = ws
                nwait += len(ws)
                ninc += 1 if o.needs_inc else 0
        self.stats = (ninc, nwait)
        print('sched stats: incs=%d waits=%d ops=%s' % (ninc, nwait, {e: len(self.ops[e]) for e in ENGS}))

    def emit(self, final_waits=()):
        nc = self.nc
        self.finalize()
        sched = self

        def replay(e, eng):
            sem_e = sched.eng_sem[e]
            for o in sched.ops[e]:
                for (sem, v) in o.waits:
                    eng.wait_ge(sem, v)
                ins = o.fn(eng)
                if o.is_dma:
                    ins.then_inc(sched.chan_sem[o.chan], 16)
                elif o.needs_inc:
                    ins.then_inc(sem_e, 1)

        with nc.Block() as block:
            @block.tensor
            def _(eng):
                replay(PE, eng)

            @block.scalar
            def _(eng):
                replay(ACT, eng)

            @block.vector
            def _(eng):
                replay(DVE, eng)

            @block.gpsimd
            def _(eng):
                replay(POOL, eng)

            @block.sync
            def _(eng):
                replay(SP, eng)
                for ch in final_waits:
                    eng.wait_ge(sched.chan_sem[ch], 16 * sched.chan_cnt[ch])


D = 2048
NKEY = 4096
CTX = 3072
OWN = 1024
HALO = 4
NE = OWN + HALO
TILES = [(0, HALO), (HALO, 512), (HALO + 512, 512)]
NH = 16
DFF = 5632
EPS = 1e-6
NEG = -30000.0
QSCALE = 192.0 ** -0.5
INTERLEAVE = True
PIPE = False

_c = {}
_o = 0
for _n, _w in [("g1", 16), ("bgate", 32), ("convw", 24), ("qag", 6), ("kvag", 4), ("gqn", 1), ("gqr", 1),
               ("gqrs", 1), ("gkn", 1), ("gkr", 1), ("gkrs", 1), ("g2", 16), ("fcw", 264), ("fcb", 88),
               ("invf", 1), ("invflo", 1), ("sgn", 1), ("kbias", 32), ("tri", 128), ("triH", 4)]:
    _c[_n] = _o
    _o += _w
CO = _c
NC_CONST = _o

KB = 1024
ARENA = 208896
A_CONST = 0
A_RING = 4 * KB
NSLOT = 4
A_UOWN = A_RING + NSLOT * 4 * KB
A_CT = A_UOWN + 32896
A_KROPE = A_CT + 32768
A_CQSQ = A_KROPE + 8192
A_SCR = A_CQSQ + 8256
A_ATT = A_SCR + 12352
A_ATTW = A_ATT + 32896


def build_program(debug=None):
    nc = bass.Bass("TRN2", target_bir_lowering=False)

    def din(name, shape, dt=F32):
        return nc.dram_tensor(name, list(shape), dt, kind="ExternalInput").ap()

    xT = din("xT", [D, NKEY])
    posb = din("posb", [64, NKEY], I32)
    cstd = din("cst", [128, NC_CONST])
    w_in = din("w_in", [D, 8512])
    wkvin = din("wkvin", [D, 640])
    wqp = din("wqp", [NH * 768, 256])
    w_kvb = din("w_kvb", [512, 4096])
    w_co = din("w_co", [1024, D])
    w_mo = din("w_mo", [D, D])
    w_o = din("w_o", [D, D])
    w_up = din("w_up", [D, 2 * DFF])
    w_dn = din("w_dn", [DFF, D])
    outT = nc.dram_tensor("outT", [D, OWN], F32, kind="ExternalOutput").ap()
    dbg = None
    dbg_items = []
    if debug is not None:
        dbg = nc.dram_tensor("dbg", [128, debug["n"]], F32, kind="ExternalOutput").ap()

    arena = nc.alloc_sbuf_tensor("arena", [128, ARENA], U8)

    def V(off, shape, dt):
        sz = 2 if dt == BF16 else 4
        nb = int(np.prod(shape)) * sz
        assert off % 4 == 0 and off + nb <= ARENA, (off, nb)
        v = arena[:, off:off + nb].bitcast(dt)
        if len(shape) == 2:
            v = v.rearrange("p (a b) -> p a b", b=shape[1])
        return v

    PW = [nc.alloc_psum_tensor("pw%d" % i, [128, 1024], F32) for i in range(2)]
    PS = [PW[0][:, 0:512], PW[0][:, 512:1024], PW[1][:, 0:512], PW[1][:, 512:1024]] + \
         [nc.alloc_psum_tensor("ps%d" % i, [128, 512], F32) for i in range(4, 8)]
    TP = [T("ps%d" % i, excl=True) for i in range(8)]

    S = Sched(nc)

    def mm(out, lhsT, rhs, start, stop, reads, writes):
        S.add(PE, lambda e: e.matmul(out, lhsT=lhsT, rhs=rhs, start=start, stop=stop), reads, writes)

    def act(out, in_, func, reads, writes, **kw):
        S.add(ACT, lambda e: e.activation(out=out, in_=in_, func=func, **kw), reads, writes)

    def stt(out, in0, scalar, in1, op0, op1, reads, writes):
        S.add(DVE, lambda e: e.scalar_tensor_tensor(out=out, in0=in0, scalar=scalar, in1=in1, op0=op0, op1=op1),
              reads, writes)

    def tt(out, in0, in1, op, reads, writes):
        S.add(DVE, lambda e: e.tensor_tensor(out=out, in0=in0, in1=in1, op=op), reads, writes)

    def ts(out, in0, s1, s2, op0, op1, reads, writes):
        if s2 is None:
            S.add(DVE, lambda e: e.tensor_scalar(out=out, in0=in0, scalar1=s1, scalar2=None, op0=op0), reads, writes)
        else:
            S.add(DVE, lambda e: e.tensor_scalar(out=out, in0=in0, scalar1=s1, scalar2=s2, op0=op0, op1=op1),
                  reads, writes)

    def cp(out, in_, reads, writes):
        S.add(DVE, lambda e: e.tensor_copy(out=out, in_=in_), reads, writes)

    def rsqrt_act(out, in_, n, reads, writes, tmp, ttmp, extra_bias=0.0):
        act(tmp, in_, AF.Ln, reads, [ttmp], scale=1.0 / n, bias=EPS)
        act(out, tmp, AF.Exp, [ttmp], writes, scale=-0.5, bias=extra_bias)

    cst = V(A_CONST, [NC_CONST], F32)
    Tcst = T("cst")
    ones = V(A_CONST + 2560, [128], BF16)
    tri = V(A_CONST + 2816, [128], BF16)
    triH = V(A_CONST + 3072, [4], BF16)
    krss = V(A_CONST + 3136, [32], F32)
    rkh = [V(A_CONST + 3264 + 128 * i, [32], F32) for i in range(2)]
    ksst = V(A_CONST + 3520, [32], F32)
    lnk = V(A_CONST + 3648, [32], F32)
    Tones, Ttri, Tkrss, Tksst, Tlnk = T("ones"), T("tri"), T("krss"), T("ksst"), T("lnk")
    Trkh = [T("rkh0"), T("rkh1")]
    gqe = V(A_CONST + 3776, [1], F32)
    Tgqe = T("gqe")

    def C(name, k=0, n=1, p=128):
        return cst[0:p, CO[name] + k:CO[name] + k + n]

    S.add(SP, lambda e: e.dma_start(out=cst, in_=cstd), writes=[Tcst], chan="cst")
    S.add(DVE, lambda e: e.memset(ones, 1.0), writes=[Tones])
    cp(tri, cst[:, CO["tri"]:CO["tri"] + 128], [Tcst], [Ttri])
    cp(triH, cst[:, CO["triH"]:CO["triH"] + 4], [Tcst], [Ttri])
    tt(gqe, cst[:, CO["gqn"]:CO["gqn"] + 1], cst[:, CO["gkn"]:CO["gkn"] + 1], ALU.mult, [Tcst], [Tgqe])

    ring = V(A_RING, [NSLOT, 2048], BF16)
    Tslot = [T("slot%d" % i) for i in range(NSLOT)]
    wblocks = []
    windex = {}

    def wdecl(key, src, r0, k, c0, n):
        windex[key] = len(wblocks)
        wblocks.append((key, src[r0:r0 + k * 128, c0:c0 + n].rearrange("(k p) n -> p k n", p=128), k, n))

    for i in range(6):
        wdecl(("qlat", i), w_in, 0, 16, 3072 + 128 * i, 128)
    for h in range(NH):
        wdecl(("hq", h), wqp, h * 768, 6, 0, 256)
        wdecl(("hkv", h), w_kvb, 0, 4, h * 256, 256)
    for i in range(8):
        wdecl(("zv", i), w_in, 0, 16, 2048 + 128 * i, 128)
        wdecl(("zc", i), w_in, 0, 16, 1024 + 128 * i, 128)
        wdecl(("zb", i), w_in, 0, 16, 128 * i, 128)
    for c in range(16):
        wdecl(("ga", c), w_in, 0, 16, 4416 + 128 * c, 128)
        wdecl(("gb", c), w_in, 0, 16, 6464 + 128 * c, 128)
        wdecl(("co", c), w_co, 0, 8, 128 * c, 128)
        wdecl(("mo", c), w_mo, 0, 16, 128 * c, 128)
    for c in range(16):
        wdecl(("wo", c), w_o, 0, 16, 128 * c, 128)
    for g in range(4):
        for jj in range(11):
            j = g * 11 + jj
            wdecl(("fg", j), w_up, 0, 16, 128 * j, 128)
            wdecl(("fu", j), w_up, 0, 16, DFF + 128 * j, 128)
        for c in range(16):
            wdecl(("fd", g, c), w_dn, g * 1408, 11, 128 * c, 128)

    wstate = {"next": 0, "done": 0}

    def wpump():
        while wstate["next"] < len(wblocks) and wstate["next"] - NSLOT < wstate["done"]:
            b = wstate["next"]
            key, src, k, n = wblocks[b]
            s = b % NSLOT
            dst = ring[:, s, 0:k * n].rearrange("p (k n) -> p k n", n=n)
            S.add(POOL, lambda e, dst=dst, src=src: e.dma_start(out=dst, in_=src), writes=[Tslot[s]], chan="w%d" % s)
            wstate["next"] += 1

    def wget(key):
        b = windex[key]
        assert b < wstate["done"] + NSLOT, key
        wpump()
        assert b < wstate["next"], key
        _, _, k, n = wblocks[b]
        s = b % NSLOT
        return ring[:, s, 0:k * n].rearrange("p (k n) -> p k n", n=n), Tslot[s]

    def wdone(key):
        b = windex[key]
        assert b == wstate["done"], (key, b, wstate["done"])
        wstate["done"] += 1
        wpump()

    dstate = {"off": 0}

    def dump(name, ap, shape, trackers):
        if debug is None or name not in debug["want"]:
            return
        n = int(np.prod(shape))
        o = dstate["off"]
        dst = dbg[0:ap.shape[0], o:o + n]
        if len(shape) == 2:
            dst = dst.rearrange("p (a b) -> p a b", b=shape[1])
        S.add(POOL, lambda e: e.dma_start(out=dst, in_=ap), reads=trackers, chan="dbg")
        dbg_items.append((name, o, n, tuple(shape), ap.shape[0]))
        dstate["off"] += n

    u_own = V(A_UOWN, [16, NE], BF16)
    Tu = {(k, t): T("u%d_%d" % (k, t)) for k in range(16) for t in range(3)}
    cT = V(A_CT, [4, NKEY], BF16)
    TcT = {(m, ci): T() for m in range(4) for ci in range(16)}
    kropeT = V(A_KROPE, [NKEY], BF16)
    Tkr = {ci: T() for ci in range(16)}
    S.add(DVE, lambda e: e.memset(kropeT[64:128, :], 0.0), writes=list(Tkr.values()))
    Cq = V(A_CQSQ, [NE], F32)
    Sq = V(A_CQSQ + 4128, [NE], F32)
    TCq = {t: T() for t in range(3)}

    NT1 = 256
    a = A_SCR
    xbuf = [V(a + i * 16384, [16, NT1], F32) for i in range(2)]
    a += 32768
    sq = V(a, [16, NT1], BF16)
    a += 8192
    uctx = [V(a + i * 8192, [16, NT1], BF16) for i in range(2)]
    a += 16384
    wkv_sb = V(a, [16, 640], BF16)
    a += 20480
    posi = [V(a + i * 1024, [NT1], I32) for i in range(2)]
    a += 2048
    rt = [V(a + i * 1024, [NT1], F32) for i in range(10)]
    a += 10240
    ni = V(a, [NT1], I32)
    a += 1024
    lnx = V(a, [NT1], F32)
    rstdx = V(a + 1024, [NT1], F32)
    lnkv = V(a + 2048, [NT1], F32)
    rstdkv = V(a + 3072, [NT1], F32)
    sqkv = V(a + 4096, [4, NT1], BF16)
    sqkr = V(a + 6144, [NT1], BF16)
    a += 6656
    assert a <= ARENA, a
    Txb = [[T() for q in range(4)] for i in range(2)]
    Tsq = [T() for q in range(4)]
    Tuc = [[T() for k in range(16)] for i in range(2)]
    Twkv = T("wkv")
    Tposi = [T(), T()]
    Trt = [T() for i in range(10)]
    Tni, Tlnx, Trstdx, Tlnkv, Trstdkv, Tsqkr = T(), T(), T(), T(), T(), T()
    Tsqkv = [T() for m in range(4)]

    S.add(POOL, lambda e: e.dma_start(out=wkv_sb, in_=wkvin.rearrange("(k p) n -> p k n", p=128)),
          writes=[Twkv], chan="wkv")
    wpump()

    INV2PI = 1.0 / (2.0 * math.pi)
    C1 = 6.28125
    C2 = 2.0 * math.pi - 6.28125
    PI_LO = 3.1415925

    def rope_tables(ci, pb, col0):
        posf, ang, tq, nf, r, Ct, St = rt[0], rt[1], rt[2], rt[3], rt[4], rt[5], rt[6]
        Tposf, Tang, Ttq, Tnf, Tr, TCt, TSt = Trt[0], Trt[1], Trt[2], Trt[3], Trt[4], Trt[5], Trt[6]
        p64 = slice(0, 64)
        cp(posf[p64], posi[pb][p64], [Tposi[pb]], [Tposf])
        ts(ang[p64], posf[p64], C("invf", p=64), None, ALU.mult, None, [Tposf, Tcst], [Tang])
        stt(ang[p64], posf[p64], C("invflo", p=64), ang[p64], ALU.mult, ALU.add, [Tposf, Tcst, Tang], [Tang])
        for which in (0, 1):
            if which == 0:
                ts(tq[p64], ang[p64], INV2PI, None, ALU.mult, None, [Tang], [Ttq])
            else:
                ts(tq[p64], ang[p64], INV2PI, 0.25, ALU.mult, ALU.add, [Tang], [Ttq])
            cp(ni[p64], tq[p64], [Ttq], [Tni])
            cp(nf[p64], ni[p64], [Tni], [Tnf])
            stt(r[p64], nf[p64], -C1, ang[p64], ALU.mult, ALU.add, [Tnf, Tang], [Tr])
            stt(r[p64], nf[p64], -C2, r[p64], ALU.mult, ALU.add, [Tnf, Tr], [Tr])
            if which == 1:
                ts(r[p64], r[p64], math.pi / 2.0, PI_LO, ALU.add, ALU.min, [Tr], [Tr])
            else:
                ts(r[p64], r[p64], PI_LO, None, ALU.min, None, [Tr], [Tr])
            ts(r[p64], r[p64], -PI_LO, None, ALU.max, None, [Tr], [Tr])
            if which == 0:
                act(St[p64], r[p64], AF.Sin, [Tr, Tcst], [TSt], scale=C("sgn", p=64))
            else:
                act(Ct[p64], r[p64], AF.Sin, [Tr], [TCt])
        return Ct, St, TCt, TSt

    def front(ci):
        col0 = ci * NT1
        b = ci % 2
        own = ci >= 12
        for q in range(4):
            src = xT[q * 512:(q + 1) * 512, col0:col0 + NT1].rearrange("(k p) n -> p k n", p=128)
            S.add(SP, lambda e, src=src, dst=xbuf[b][:, 4 * q:4 * q + 4, :]: e.dma_start(out=dst, in_=src),
                  writes=[Txb[b][q]], chan="x%d%d" % (b, q))
        S.add(SP, lambda e, dst=posi[b][0:64], src=posb[:, col0:col0 + NT1]: e.dma_start(out=dst, in_=src),
              writes=[Tposi[b]], chan="pos%d" % b)
        for q in range(4):
            act(sq[:, 4 * q:4 * q + 4, :], xbuf[b][:, 4 * q:4 * q + 4, :], AF.Square, [Txb[b][q]], [Tsq[q]])
        for k in range(16):
            mm(PS[0][:, 0:NT1], ones, sq[:, k, :], k == 0, k == 15, [Tones, Tsq[k // 4]], [TP[0]])
        rsqrt_act(rstdx, PS[0][:, 0:NT1], float(D), [TP[0]], [Trstdx], lnx, Tlnx)
        for k in range(16):
            if own:
                t = 1 + (ci - 12) // 2
                off = TILES[t][0] + ((ci - 12) % 2) * NT1
                dst, Tdst = u_own[:, k, off:off + NT1], Tu[(k, t)]
            else:
                dst, Tdst = uctx[b][:, k, :], Tuc[b][k]
            stt(dst, xbuf[b][:, k, :], C("g1", k), rstdx, ALU.mult, ALU.mult, [Txb[b][k // 4], Tcst, Trstdx], [Tdst])
        if ci == 11:
            cp(u_own[:, :, 0:HALO], uctx[b][:, :, NT1 - HALO:NT1], Tuc[b], [Tu[(k, 0)] for k in range(16)])

    def back(ci):
        col0 = ci * NT1
        b = ci % 2
        own = ci >= 12

        def usrc(k):
            if own:
                t = 1 + (ci - 12) // 2
                off = TILES[t][0] + ((ci - 12) % 2) * NT1
                return u_own[:, k, off:off + NT1], Tu[(k, t)]
            return uctx[b][:, k, :], Tuc[b][k]

        for m in range(4):
            for k in range(16):
                ua, Tua = usrc(k)
                mm(PS[1 + m][:, 0:NT1], wkv_sb[:, k, m * 128:(m + 1) * 128], ua, k == 0, k == 15, [Twkv, Tua], [TP[1 + m]])
        for k in range(16):
            ua, Tua = usrc(k)
            mm(PS[5][0:64, 0:NT1], wkv_sb[:, k, 512:576], ua, k == 0, k == 15, [Twkv, Tua], [TP[5]])
        for k in range(16):
            ua, Tua = usrc(k)
            mm(PS[6][0:64, 0:NT1], wkv_sb[:, k, 576:640], ua, k == 0, k == 15, [Twkv, Tua], [TP[6]])
        for m in range(4):
            act(sqkv[:, m, :], PS[1 + m][:, 0:NT1], AF.Square, [TP[1 + m]], [Tsqkv[m]])
        for m in range(4):
            mm(PS[7][:, 0:NT1], ones, sqkv[:, m, :], m == 0, m == 3, [Tones, Tsqkv[m]], [TP[7]])
        rsqrt_act(rstdkv, PS[7][:, 0:NT1], 512.0, [TP[7]], [Trstdkv], lnkv, Tlnkv)
        for m in range(4):
            stt(cT[:, m, col0:col0 + NT1], PS[1 + m][:, 0:NT1], C("kvag", m), rstdkv, ALU.mult, ALU.mult,
                [TP[1 + m], Tcst, Trstdkv], [TcT[(m, ci)]])
        Ct, St, TCt, TSt = rope_tables(ci, b, col0)
        p64 = slice(0, 64)
        if own or ci == 11:
            if own:
                t = 1 + (ci - 12) // 2
                off = TILES[t][0] + ((ci - 12) % 2) * NT1
                srcs = slice(0, NT1)
                n = NT1
            else:
                t, off, srcs, n = 0, 0, slice(NT1 - HALO, NT1), HALO
            ts(Cq[p64, off:off + n], Ct[p64, srcs], C("gqr", p=64), None, ALU.mult, None, [TCt, Tcst], [TCq[t]])
            ts(Sq[p64, off:off + n], St[p64, srcs], C("gqrs", p=64), None, ALU.mult, None, [TSt, Tcst], [TCq[t]])
        Ck, Sk, t1, t2 = rt[7], rt[8], rt[0], rt[2]
        TCk, TSk, Tt1, Tt2 = Trt[7], Trt[8], Trt[0], Trt[2]
        ts(Ck[p64], Ct[p64], C("gkr", p=64), None, ALU.mult, None, [TCt, Tcst], [TCk])
        ts(Sk[p64], St[p64], C("gkrs", p=64), None, ALU.mult, None, [TSt, Tcst], [TSk])
        tt(t1[p64], PS[5][0:64, 0:NT1], Ck[p64], ALU.mult, [TP[5], TCk], [Tt1])
        tt(t2[p64], PS[6][0:64, 0:NT1], Sk[p64], ALU.mult, [TP[6], TSk], [Tt2])
        tt(kropeT[0:64, col0:col0 + NT1], t1[p64], t2[p64], ALU.add, [Tt1, Tt2], [Tkr[ci]])
        act(sqkr[0:64], PS[5][0:64, 0:NT1], AF.Square, [TP[5]], [Tsqkr])
        for bb in range(2):
            mm(PS[7][:, 256 + bb:257 + bb], sqkr[0:64, bb * 128:(bb + 1) * 128], ones[0:64, 0:1], True, True,
               [Tsqkr, Tones], [TP[7]])
        cp(krss[:, 2 * ci:2 * ci + 2], PS[7][:, 256:258], [TP[7]], [Tkrss])

    front(0)
    for ci in range(16):
        if ci + 1 < 16:
            front(ci + 1)
        back(ci)

    dump("u_own", u_own, [16, NE], list(Tu.values()))
    dump("cT", cT, [4, NKEY], list(TcT.values()))
    dump("kropeT", kropeT[0:64], [NKEY], list(Tkr.values()))
    dump("krss", krss, [32], [Tkrss])
    dump("Cq", Cq[0:64], [NE], list(TCq.values()))
    dump("Sq", Sq[0:64], [NE], list(TCq.values()))
    S.barrier()
    if debug is not None and debug.get("stop") == "1a":
        return finish(nc, S, outT, dbg_items, debug)

    qn = V(A_SCR, [6, NE], BF16)
    Tqn = {(i, t): T() for i in range(6) for t in range(3)}
    a = A_SCR + 12352
    qlatf = V(a, [6, NE], F32)
    a += 24704
    sqq = V(a, [6, NE], BF16)
    a += 12352
    lnq = V(a, [NE], F32)
    a += 4160
    rstdq = V(a, [NE], F32)
    a += 4160
    Tqlf = {(i, t): T() for i in range(6) for t in range(3)}
    Tsqq = {(i, t): T() for i in range(6) for t in range(3)}
    Tlnq = [T() for t in range(3)]
    Trq = [T() for t in range(3)]
    pi = 0
    for i in range(6):
        wv, Tw = wget(("qlat", i))
        for t, (off, n) in enumerate(TILES):
            p = pi % 6
            pi += 1
            for k in range(16):
                mm(PS[p][:, 0:n], wv[:, k, :], u_own[:, k, off:off + n], k == 0, k == 15, [Tw, Tu[(k, t)]], [TP[p]])
            act(qlatf[:, i, off:off + n], PS[p][:, 0:n], AF.Copy, [TP[p]], [Tqlf[(i, t)]])
            act(sqq[:, i, off:off + n], PS[p][:, 0:n], AF.Square, [TP[p]], [Tsqq[(i, t)]])
        wdone(("qlat", i))
    for t, (off, n) in enumerate(TILES):
        p = 6 + (t % 2)
        for i in range(6):
            mm(PS[p][:, 0:n], ones, sqq[:, i, off:off + n], i == 0, i == 5, [Tones, Tsqq[(i, t)]], [TP[p]])
        rsqrt_act(rstdq[:, off:off + n], PS[p][:, 0:n], 768.0, [TP[p]], [Trq[t]], lnq[:, off:off + n], Tlnq[t])
        for i in range(6):
            stt(qn[:, i, off:off + n], qlatf[:, i, off:off + n], C("qag", i), rstdq[:, off:off + n], ALU.mult, ALU.mult,
                [Tqlf[(i, t)], Tcst, Trq[t]], [Tqn[(i, t)]])
    dump("qn", qn, [6, NE], list(Tqn.values()))
    S.barrier()
    if debug is not None and debug.get("stop") == "1b":
        return finish(nc, S, outT, dbg_items, debug)

    attnT = V(A_ATT, [16, NE], BF16)
    Tat = {(h, t): T() for h in range(NH) for t in range(3)}
    a = A_ATTW
    Kn = [V(a + i * 8192, [NKEY], BF16) for i in range(2)]
    a += 16384
    Vh = [V(a + i * 8192, [32, 128], BF16) for i in range(2)]
    a += 16384
    Qn = [V(a + i * 2112, [NE], BF16) for i in range(2)]
    a += 4224
    Qr = [V(a + i * 2112, [NE], BF16) for i in range(2)]
    a += 4224
    PT = [V(a + i * 2048, [1024], BF16) for i in range(3)]
    a += 6144
    sqk = [V(a + i * 1024, [512], BF16) for i in range(2)]
    a += 2048
    sqn = V(a, [512], BF16)
    a += 1024
    sqr = V(a, [512], BF16)
    a += 1024
    rstdqh = V(a, [256], F32)
    a += 1024
    qt1 = V(a, [256], F32)
    a += 1024
    qt2 = V(a, [256], F32)
    a += 1024
    acc = V(a, [1024], F32)
    a += 4096
    rl = acc
    accb = V(a, [1024], BF16)
    a += 2048
    assert a <= ARENA, a
    TKn = [[T() for ct in range(8)] for i in range(2)]
    TVh = [[T() for g in range(8)] for i in range(2)]
    TQn = [[T() for t in range(3)] for i in range(2)]
    TQr = [[T() for t in range(3)] for i in range(2)]
    TPT = [T() for i in range(3)]
    Tsqk = [T(), T()]
    Tsqn, Tsqr, Trstdqh, Tqt1, Tqt2, Tacc, Taccb = T(), T(), T(), T(), T(), T(), T()
    Trl = Tacc
    TcTcol = lambda m, c0, n: [TcT[(m, ci)] for ci in range(c0 // 256, (c0 + n + 255) // 256)]
    LNQS = math.log(QSCALE)
    for i_ in range(2):
        S.add(DVE, lambda e, i_=i_: e.memset(Qr[i_][64:128, :], 0.0), writes=TQr[i_])
    pti = 0
    sti = 0

    def prologue_units(h):
        hb = h % 2
        st = {}
        units = []

        def u_start():
            st["wq"], st["Twq"] = wget(("hq", h))
            st["wkv"], st["Twkv"] = wget(("hkv", h))

        def KU_stat(ct):
            sb = ct % 2
            for bb in range(4):
                mm(PS[6][:, 4 * ct + bb:4 * ct + bb + 1], sqk[sb][:, bb * 128:(bb + 1) * 128], ones[:, 0:1], True, True,
                   [Tsqk[sb], Tones], [TP[6]])

        def KU(ct):
            wkv, Twkvh = st["wkv"], st["Twkv"]
            p = 4 + (ct % 2)
            c0 = ct * 512
            for k in range(4):
                mm(PS[p][:, :], wkv[:, k, 0:128], cT[:, k, c0:c0 + 512], k == 0, k == 3,
                   [Twkvh] + TcTcol(k, c0, 512), [TP[p]])
            sb = ct % 2
            cp(Kn[hb][:, c0:c0 + 512], PS[p][:, :], [TP[p]], [TKn[hb][ct]])
            act(sqk[sb], Kn[hb][:, c0:c0 + 512], AF.Square, [TKn[hb][ct]], [Tsqk[sb]])
            if ct > 0:
                KU_stat(ct - 1)

        def KF():
            KU_stat(7)
            tt(ksst, PS[6][:, 0:32], krss, ALU.add, [TP[6], Tkrss], [Tksst])
            rsqrt_act(rkh[hb], ksst, 192.0, [Tksst], [Trkh[hb]], lnk, Tlnk, extra_bias=LNQS)

        def VU(g):
            wkv, Twkvh = st["wkv"], st["Twkv"]
            p = 6 + (g % 2)
            for bb in range(4):
                kb = 4 * g + bb
                for k in range(4):
                    mm(PS[p][:, bb * 128:(bb + 1) * 128], cT[:, k, kb * 128:(kb + 1) * 128], wkv[:, k, 128:256],
                       k == 0, k == 3, [Twkvh] + TcTcol(k, kb * 128, 128), [TP[p]])
            cp(Vh[hb][:, 4 * g:4 * g + 4, :], PS[p][:, :].rearrange("p (a b) -> p a b", b=128), [TP[p]], [TVh[hb][g]])

        def QU1(t):
            wq, Twq = st["wq"], st["Twq"]
            off, n = TILES[t]
            for k in range(6):
                mm(PS[5][:, 0:n], wq[:, k, 0:128], qn[:, k, off:off + n], k == 0, k == 5, [Twq, Tqn[(k, t)]], [TP[5]])
            for k in range(6):
                mm(PS[6][0:64, 0:n], wq[:, k, 128:192], qn[:, k, off:off + n], k == 0, k == 5, [Twq, Tqn[(k, t)]], [TP[6]])
            for k in range(6):
                mm(PS[7][0:64, 0:n], wq[:, k, 192:256], qn[:, k, off:off + n], k == 0, k == 5, [Twq, Tqn[(k, t)]], [TP[7]])
            act(sqn[:, 0:n], PS[5][:, 0:n], AF.Square, [TP[5]], [Tsqn])
            act(sqr[0:64, 0:n], PS[6][0:64, 0:n], AF.Square, [TP[6]], [Tsqr])

        def QU2(t):
            off, n = TILES[t]
            mm(PS[4][:, 0:n], ones, sqn[:, 0:n], True, False, [Tones, Tsqn], [TP[4]])
            mm(PS[4][:, 0:n], ones[0:64, :], sqr[0:64, 0:n], False, True, [Tones, Tsqr], [TP[4]])
            for c0 in range(0, n, 256):
                m = min(256, n - c0)
                ps_, pe_ = c0, c0 + m
                rsqrt_act(rstdqh[:, 0:m], PS[4][:, ps_:pe_], 192.0, [TP[4]], [Trstdqh], rstdqh[:, 0:m], Trstdqh)
                stt(Qn[hb][:, off + ps_:off + pe_], PS[5][:, ps_:pe_], gqe, rstdqh[:, 0:m], ALU.mult, ALU.mult,
                    [TP[5], Tgqe, Trstdqh], [TQn[hb][t]])
                tt(qt1[0:64, 0:m], PS[6][0:64, ps_:pe_], Cq[0:64, off + ps_:off + pe_], ALU.mult, [TP[6], TCq[t]], [Tqt1])
                tt(qt2[0:64, 0:m], PS[7][0:64, ps_:pe_], Sq[0:64, off + ps_:off + pe_], ALU.mult, [TP[7], TCq[t]], [Tqt2])
                tt(qt1[0:64, 0:m], qt1[0:64, 0:m], qt2[0:64, 0:m], ALU.add, [Tqt1, Tqt2], [Tqt1])
                tt(Qr[hb][0:64, off + ps_:off + pe_], qt1[0:64, 0:m], rstdqh[0:64, 0:m], ALU.mult, [Tqt1, Trstdqh],
                   [TQr[hb][t]])

        def u_end():
            wdone(("hq", h))
            wdone(("hkv", h))

        units.append(u_start)
        for ct in range(8):
            units.append(lambda ct=ct: KU(ct))
        units.append(KF)
        for t in range(3):
            units.append(lambda t=t: QU1(t))
            units.append(lambda t=t: QU2(t))
        vunits = [(lambda g=g: VU(g)) for g in range(8)]
        return units, vunits, u_end

    hsteps = [(0, 0, HALO, kb, 24, kb - 23, 0) for kb in range(24)]
    SBANK = [0, 1, 2]

    def emit_qk(h, step):
        nonlocal sti, pti
        hb = h % 2
        t, off, n, kb, nblk, d, lo = step
        sp = SBANK[sti % 3]
        sti += 1
        pt = pti % 3
        pti += 1
        mm(PS[sp][:, 0:n], Kn[hb][:, kb * 128:(kb + 1) * 128], Qn[hb][:, off:off + n], True, False,
           [TKn[hb][kb // 4], TQn[hb][t]], [TP[sp]])
        mm(PS[sp][:, 0:n], kropeT[:, kb * 128:(kb + 1) * 128], Qr[hb][:, off:off + n], False, True,
           [Tkr[kb // 2], TQr[hb][t]], [TP[sp]])
        act(PT[pt][:, 0:n], PS[sp][:, 0:n], AF.Exp, [TP[sp], Trkh[hb], Tcst], [TPT[pt]],
            scale=rkh[hb][:, kb:kb + 1], bias=C("kbias", kb))
        if d >= 0:
            tt(PT[pt][:, 0:HALO], PT[pt][:, 0:HALO], triH, ALU.mult, [TPT[pt], Ttri], [TPT[pt]])
        return pt

    def emit_pv(h, step, pt):
        nonlocal sti
        hb = h % 2
        t, off, n, kb, nblk, d, lo = step
        po = 3
        mm(PS[po][:, 0:n], Vh[hb][:, kb, :], PT[pt][:, 0:n], kb == 0, kb == nblk - 1, [TVh[hb][kb // 4], TPT[pt]], [TP[po]])
        if kb == 0:
            cp(acc[:, 0:n], PT[pt][:, 0:n], [TPT[pt]], [Tacc])
        else:
            tt(acc[:, 0:n], acc[:, 0:n], PT[pt][:, 0:n], ALU.add, [Tacc, TPT[pt]], [Tacc])
        if kb == nblk - 1:
            cp(accb[:, 0:n], acc[:, 0:n], [Tacc], [Taccb])
            sp = SBANK[sti % 3]
            sti += 1
            mm(PS[sp][:, 0:n], ones, accb[:, 0:n], True, True, [Tones, Taccb], [TP[sp]])
            act(rl[:, 0:n], PS[sp][:, 0:n], AF.Ln, [TP[sp]], [Trl], bias=1e-30)
            act(rl[:, 0:n], rl[:, 0:n], AF.Exp, [Trl], [Trl], scale=-1.0)
            tt(attnT[:, h, off:off + n], PS[po][:, 0:n], rl[:, 0:n], ALU.mult, [TP[po], Trl], [Tat[(h, t)]])

    C0 = HALO
    dsteps = []
    for kb in range(32):
        if kb < 24:
            dsteps.append((kb, 0, 0))
        elif kb < 28:
            dsteps.append((kb, 128 * (kb - 24), 128 * (kb - 24)))
        else:
            dsteps.append((kb, 128 * (kb - 28), 512 + 128 * (kb - 28)))
    dcnt = [0]

    def emit_qk2(h, dstep):
        hb = h % 2
        kb, lo, c_lo = dstep
        x = dcnt[0] % 2
        y = dcnt[0] % 3
        dcnt[0] += 1
        pw = PW[x]
        Tpw = [TP[2 * x], TP[2 * x + 1]]
        kblk = slice(kb * 128, (kb + 1) * 128)
        if kb < 28:
            mm(pw[:, c_lo:512], Kn[hb][:, kblk], Qn[hb][:, C0 + c_lo:C0 + 512], True, False,
               [TKn[hb][kb // 4], TQn[hb][1]], [Tpw[0]])
            mm(pw[:, c_lo:512], kropeT[:, kblk], Qr[hb][:, C0 + c_lo:C0 + 512], False, True,
               [Tkr[kb // 2], TQr[hb][1]], [Tpw[0]])
        c1 = max(c_lo, 512)
        mm(pw[:, c1:1024], Kn[hb][:, kblk], Qn[hb][:, C0 + c1:C0 + 1024], True, False,
           [TKn[hb][kb // 4], TQn[hb][2]], [Tpw[1]])
        mm(pw[:, c1:1024], kropeT[:, kblk], Qr[hb][:, C0 + c1:C0 + 1024], False, True,
           [Tkr[kb // 2], TQr[hb][2]], [Tpw[1]])
        rd = Tpw if kb < 28 else [Tpw[1]]
        act(PT[y][:, c_lo:1024], pw[:, c_lo:1024], AF.Exp, rd + [Trkh[hb], Tcst], [TPT[y]],
            scale=rkh[hb][:, kb:kb + 1], bias=C("kbias", kb))
        if kb >= 24:
            tt(PT[y][:, c_lo:c_lo + 128], PT[y][:, c_lo:c_lo + 128], tri, ALU.mult, [TPT[y], Ttri], [TPT[y]])
        return y

    def emit_pv2(h, dstep, x):
        hb = h % 2
        kb, lo, c_lo = dstep
        if kb < 28:
            mm(PS[4][:, c_lo:512], Vh[hb][:, kb, :], PT[x][:, c_lo:512], kb == 0, kb == 27,
               [TVh[hb][kb // 4], TPT[x]], [TP[4]])
        c1 = max(c_lo, 512)
        mm(PS[5][:, c1 - 512:512], Vh[hb][:, kb, :], PT[x][:, c1:1024], kb == 0, kb == 31,
           [TVh[hb][kb // 4], TPT[x]], [TP[5]])
        if kb == 0:
            cp(acc[:, 0:1024], PT[x][:, 0:1024], [TPT[x]], [Tacc])
        else:
            tt(acc[:, c_lo:1024], acc[:, c_lo:1024], PT[x][:, c_lo:1024], ALU.add, [Tacc, TPT[x]], [Tacc])
        if kb == 31:
            cp(accb[:, 0:1024], acc[:, 0:1024], [Tacc], [Taccb])
            for half in range(2):
                hs = slice(512 * half, 512 * half + 512)
                mm(PS[6 + half][:, :], ones, accb[:, hs], True, True, [Tones, Taccb], [TP[6 + half]])
                act(rl[:, hs], PS[6 + half][:, :], AF.Ln, [TP[6 + half]], [Trl], bias=1e-30)
            act(rl[:, 0:1024], rl[:, 0:1024], AF.Exp, [Trl], [Trl], scale=-1.0)
            for half in range(2):
                hs = slice(512 * half, 512 * half + 512)
                tt(attnT[:, h, C0 + 512 * half:C0 + 512 * half + 512], PS[4 + half][:, :], rl[:, hs], ALU.mult,
                   [TP[4 + half], Trl], [Tat[(h, 1 + half)]])

    u0, v0, e0 = prologue_units(0)
    for u in u0 + v0 + [e0]:
        u()
    for h in range(NH):
        if h + 1 < NH:
            units, vunits, u_end = prologue_units(h + 1)
        else:
            units, vunits, u_end = [], [], None
        ns = len(hsteps)
        pts = [None] * ns
        pts[0] = emit_qk(h, hsteps[0])
        pts[1] = emit_qk(h, hsteps[1])
        for i in range(ns):
            if i + 2 < ns:
                pts[i + 2] = emit_qk(h, hsteps[i + 2])
            emit_pv(h, hsteps[i], pts[i])
            if units and i % 3 != 2:
                units.pop(0)()
        while units:
            units.pop(0)()
        nd = len(dsteps)
        xs = [None] * nd
        xs[0] = emit_qk2(h, dsteps[0])
        for i in range(nd):
            if i + 1 < nd:
                xs[i + 1] = emit_qk2(h, dsteps[i + 1])
            if vunits and i % 3 == 1:
                vunits.pop(0)()
            emit_pv2(h, dsteps[i], xs[i])
        while vunits:
            vunits.pop(0)()
        if u_end is not None:
            u_end()
        if h == 0:
            dump("Kn0", Kn[0], [NKEY], TKn[0])
            dump("Vh0", Vh[0], [32, 128], TVh[0])
            dump("Qn0", Qn[0], [NE], TQn[0])
            dump("Qr0", Qr[0][0:64], [NE], TQr[0])
            dump("rk0", rkh[0], [32], [Trkh[0]])
    dump("attnT", attnT, [16, NE], list(Tat.values()))
    S.barrier()
    if debug is not None and debug.get("stop") == "2":
        return finish(nc, S, outT, dbg_items, debug)

    gconvT = V(A_CT, [8, NE], BF16)
    Tgc = {i: T() for i in range(8)}
    a = A_CT + 16448
    cv = V(a, [NE], F32)
    zvs = V(a + 4160, [NE], F32)
    zbs = V(a + 8320, [NE], F32)
    yc = V(a + 12480, [NE], F32)
    A_S4 = a + 16640
    Tcv, Tzvs, Tzbs, Tyc = T(), T(), T(), T()
    S.add(DVE, lambda e: e.memset(gconvT[:, :, 0:2], 0.0), writes=list(Tgc.values()))
    for i in range(8):
        pi3 = 0
        for (key, kind) in ((("zv", i), "v"), (("zc", i), "c"), (("zb", i), "b")):
            wx, Twx = wget(key)
            for t, (off, n) in enumerate(TILES):
                p = pi3 % 6
                pi3 += 1
                for k in range(16):
                    mm(PS[p][:, 0:n], wx[:, k, :], u_own[:, k, off:off + n], k == 0, k == 15, [Twx, Tu[(k, t)]], [TP[p]])
                if kind == "v":
                    act(zvs[:, off:off + n], PS[p][:, 0:n], AF.Copy, [TP[p]], [Tzvs])
                elif kind == "c":
                    tt(cv[:, off:off + n], PS[p][:, 0:n], zvs[:, off:off + n], ALU.mult, [TP[p], Tzvs], [Tcv])
                else:
                    act(zbs[:, off:off + n], PS[p][:, 0:n], AF.Copy, [TP[p]], [Tzbs])
            wdone(key)
        m = NE - 2
        ts(yc[:, 2:NE], cv[:, 2:NE], C("convw", 3 * i + 2), None, ALU.mult, None, [Tcv, Tcst], [Tyc])
        stt(yc[:, 2:NE], cv[:, 1:NE - 1], C("convw", 3 * i + 1), yc[:, 2:NE], ALU.mult, ALU.add, [Tcv, Tcst, Tyc], [Tyc])
        stt(yc[:, 2:NE], cv[:, 0:NE - 2], C("convw", 3 * i + 0), yc[:, 2:NE], ALU.mult, ALU.add, [Tcv, Tcst, Tyc], [Tyc])
        tt(gconvT[:, i, 2:NE], yc[:, 2:NE], zbs[:, 2:NE], ALU.mult, [Tyc, Tzbs], [Tgc[i]])
    dump("gconvT", gconvT, [8, NE], list(Tgc.values()))

    mT = V(A_ATTW, [16, NE], BF16)
    Tm = {(c, t): T() for c in range(16) for t in range(3)}
    a = A_S4
    sga = V(a, [NE], F32)
    sgb = V(a + 4160, [NE], F32)
    mt1 = V(a + 8320, [512], F32)
    mt2 = V(a + 8320 + 2048, [512], F32)
    assert a + 8320 + 4096 <= A_ATT
    Tsga = [T() for t in range(3)]
    Tsgb = [T() for t in range(3)]
    Tmt1, Tmt2 = T(), T()
    for c in range(16):
        wga, Twga = wget(("ga", c))
        wgb, Twgb = wget(("gb", c))
        for t, (off, n) in enumerate(TILES):
            pa, pb_ = (0, 1) if t % 2 == 0 else (2, 3)
            for k in range(16):
                mm(PS[pa][:, 0:n], wga[:, k, :], u_own[:, k, off:off + n], k == 0, k == 15, [Twga, Tu[(k, t)]], [TP[pa]])
            for k in range(16):
                mm(PS[pb_][:, 0:n], wgb[:, k, :], u_own[:, k, off:off + n], k == 0, k == 15, [Twgb, Tu[(k, t)]], [TP[pb_]])
            act(sga[:, off:off + n], PS[pa][:, 0:n], AF.Sigmoid, [TP[pa], Tcst], [Tsga[t]], bias=C("bgate", c))
            act(sgb[:, off:off + n], PS[pb_][:, 0:n], AF.Sigmoid, [TP[pb_], Tcst], [Tsgb[t]], bias=C("bgate", 16 + c))
        wdone(("ga", c))
        wdone(("gb", c))
        wco, Twco = wget(("co", c))
        wmo, Twmo = wget(("mo", c))
        for t, (off, n) in enumerate(TILES):
            pa, pb_ = (4, 5) if t % 2 == 0 else (6, 7)
            for k in range(8):
                mm(PS[pa][:, 0:n], wco[:, k, :], gconvT[:, k, off:off + n], k == 0, k == 7, [Twco, Tgc[k]], [TP[pa]])
            for k in range(16):
                mm(PS[pb_][:, 0:n], wmo[:, k, :], attnT[:, k, off:off + n], k == 0, k == 15, [Twmo, Tat[(k, t)]], [TP[pb_]])
            tt(mt1[:, 0:n], PS[pa][:, 0:n], sga[:, off:off + n], ALU.mult, [TP[pa], Tsga[t]], [Tmt1])
            tt(mt2[:, 0:n], PS[pb_][:, 0:n], sgb[:, off:off + n], ALU.mult, [TP[pb_], Tsgb[t]], [Tmt2])
            tt(mT[:, c, off:off + n], mt1[:, 0:n], mt2[:, 0:n], ALU.add, [Tmt1, Tmt2], [Tm[(c, t)]])
        wdone(("co", c))
        wdone(("mo", c))
    dump("mT", mT, [16, NE], list(Tm.values()))
    S.barrier()

    hres = V(A_UOWN, [16, NE], F32)
    Th = {(c, t): T() for c in range(16) for t in range(3)}
    a = A_UOWN + 65792
    sq2 = V(a, [16, NE], BF16)
    Tsq2 = {(c, t): T() for c in range(16) for t in range(3)}
    u2 = V(a, [16, NE], BF16)
    a += 32896
    ln2 = V(a, [NE], F32)
    rstd2 = V(a + 4160, [NE], F32)
    a += 8320
    A_S7 = a
    assert a <= A_ATTW
    Tln2 = [T() for t in range(3)]
    Trstd2 = [T() for t in range(3)]
    for c in range(16):
        src = xT[c * 128:(c + 1) * 128, CTX - HALO:NKEY]
        Tc3 = [Th[(c, t)] for t in range(3)]
        S.add(SP, lambda e, src=src, dst=hres[:, c, :]: e.dma_start(out=dst, in_=src), writes=Tc3, chan="xr%d" % c)
    for c in range(16):
        wo, Two = wget(("wo", c))
        for t, (off, n) in enumerate(TILES):
            p = (3 * c + t) % 5
            for k in range(16):
                mm(PS[p][:, 0:n], wo[:, k, :], mT[:, k, off:off + n], k == 0, k == 15, [Two, Tm[(k, t)]], [TP[p]])
            tt(hres[:, c, off:off + n], PS[p][:, 0:n], hres[:, c, off:off + n], ALU.add, [TP[p], Th[(c, t)]], [Th[(c, t)]])
            act(sq2[:, c, off:off + n], hres[:, c, off:off + n], AF.Square, [Th[(c, t)]], [Tsq2[(c, t)]])
        wdone(("wo", c))
    for t, (off, n) in enumerate(TILES):
        p = 5 + t
        for c in range(16):
            mm(PS[p][:, 0:n], ones, sq2[:, c, off:off + n], c == 0, c == 15, [Tones, Tsq2[(c, t)]], [TP[p]])
        rsqrt_act(rstd2[:, off:off + n], PS[p][:, 0:n], float(D), [TP[p]], [Trstd2[t]], ln2[:, off:off + n], Tln2[t])
    dump("h1", hres, [16, NE], list(Th.values()))
    S.barrier()

    Tu2 = {(k, t): T() for k in range(16) for t in range(3)}
    for k in range(16):
        for t, (off, n) in enumerate(TILES):
            stt(u2[:, k, off:off + n], hres[:, k, off:off + n], C("g2", k), rstd2[:, off:off + n], ALU.mult, ALU.mult,
                [Th[(k, t)], Tcst, Trstd2[t]], [Tu2[(k, t)]])
    dump("u2", u2, [16, NE], list(Tu2.values()))

    a = A_S7
    Ag = [V(a + i * 4160, [NE], F32) for i in range(2)]
    a += 8320
    Au = [V(a + i * 4160, [NE], F32) for i in range(2)]
    a += 8320
    yg = V(a, [OWN], F32)
    yu = V(a + 4096, [OWN], F32)
    a += 8192
    fT = [V(a + i * 22528, [11, OWN], BF16) for i in range(2)]
    a += 45056
    assert a <= ARENA, a
    TAg = [[T() for t in range(3)] for i in range(2)]
    TAu = [[T() for t in range(3)] for i in range(2)]
    Tyg, Tyu = T(), T()
    TfT = [[T() for jj in range(11)] for i in range(2)]
    H0 = HALO
    for g in range(4):
        gb = g % 2
        for jj in range(11):
            j = g * 11 + jj
            ab = j % 2
            wg_, Twg = wget(("fg", j))
            wu_, Twu = wget(("fu", j))
            for t, (off, n) in ((1, TILES[1]), (2, TILES[2]), (0, TILES[0])):
                pg, pu = {1: (0, 1), 2: (2, 3), 0: (4, 5)}[t]
                for k in range(16):
                    mm(PS[pg][:, 0:n], wg_[:, k, :], u2[:, k, off:off + n], k == 0, k == 15, [Twg, Tu2[(k, t)]], [TP[pg]])
                for k in range(16):
                    mm(PS[pu][:, 0:n], wu_[:, k, :], u2[:, k, off:off + n], k == 0, k == 15, [Twu, Tu2[(k, t)]], [TP[pu]])
                act(Ag[ab][:, off:off + n], PS[pg][:, 0:n], AF.Copy, [TP[pg]], [TAg[ab][t]])
                act(Au[ab][:, off:off + n], PS[pu][:, 0:n], AF.Copy, [TP[pu]], [TAu[ab][t]])
            wdone(("fg", j))
            wdone(("fu", j))
            for (A_, TA_, y_, Ty_, jc) in ((Ag[ab], TAg[ab], yg, Tyg, j), (Au[ab], TAu[ab], yu, Tyu, 44 + j)):
                ts(y_, A_[:, H0:NE], C("fcw", 3 * jc + 2), C("fcb", jc), ALU.mult, ALU.add, TA_ + [Tcst], [Ty_])
                stt(y_, A_[:, H0 - 1:NE - 1], C("fcw", 3 * jc + 1), y_, ALU.mult, ALU.add, TA_ + [Tcst, Ty_], [Ty_])
                stt(y_, A_[:, H0 - 2:NE - 2], C("fcw", 3 * jc + 0), y_, ALU.mult, ALU.add, TA_ + [Tcst, Ty_], [Ty_])
            act(yg, yg, AF.Silu, [Tyg], [Tyg])
            tt(fT[gb][:, jj, :], yg, yu, ALU.mult, [Tyg, Tyu], [TfT[gb][jj]])
        for c in range(16):
            wd, Twd = wget(("fd", g, c))
            for t in (1, 2):
                off, n = TILES[t]
                p = 6 + (t % 2)
                for jj in range(11):
                    mm(PS[p][:, 0:n], wd[:, jj, :], fT[gb][:, jj, off - H0:off - H0 + n], jj == 0, jj == 10,
                       [Twd, TfT[gb][jj]], [TP[p]])
                tt(hres[:, c, off:off + n], PS[p][:, 0:n], hres[:, c, off:off + n], ALU.add, [TP[p], Th[(c, t)]], [Th[(c, t)]])
            wdone(("fd", g, c))
            if g == 3:
                S.add(SP, lambda e, dst=outT[c * 128:(c + 1) * 128, :], src=hres[:, c, H0:NE]: e.dma_start(out=dst, in_=src),
                      reads=[Th[(c, 1)], Th[(c, 2)]], chan="out%d" % c)
    return finish(nc, S, outT, dbg_items, debug, outs=True)


def finish(nc, S, outT, dbg_items, debug, outs=False):
    fw = []
    if outs:
        fw += ["out%d" % i for i in range(16)]
    if debug is not None and "dbg" in S.chan_cnt:
        fw.append("dbg")
    if debug is not None and not outs:
        pass
    S.emit(final_waits=fw)
    return nc, dbg_items


def _chunked(v, nchunk):
    return np.ascontiguousarray(v.reshape(nchunk, 128).T)


def prepare_inputs(x, positions, ln1_g, w_in, b_gate, conv_w, w_conv_out, q_a_g, w_q_b, kv_a_g, w_kv_b,
                   q_norm_g, k_norm_g, w_mla_out, w_o, ln2_g, w_ffn_up, ffn_conv_w, ffn_conv_b, w_ffn_down):
    f32 = np.float32
    x = np.asarray(x, f32)
    positions = np.asarray(positions)
    w_in0 = np.ascontiguousarray(np.asarray(w_in, f32)[0])
    wq0 = np.asarray(w_q_b, f32)[0]
    kr0 = 4352
    wkvin = np.ascontiguousarray(np.concatenate(
        [w_in0[:, 3840:4416], w_in0[:, kr0 + 32:kr0 + 64], w_in0[:, kr0:kr0 + 32]], axis=1))
    wqp = np.empty((NH, 768, 256), f32)
    for h in range(NH):
        b0 = h * 192
        wqp[h, :, 0:128] = wq0[:, b0:b0 + 128]
        wqp[h, :, 128:192] = wq0[:, b0 + 128:b0 + 192]
        wqp[h, :, 192:224] = wq0[:, b0 + 160:b0 + 192]
        wqp[h, :, 224:256] = wq0[:, b0 + 128:b0 + 160]
    wqp = wqp.reshape(NH * 768, 256)
    shared = {
        "w_in": w_in0,
        "wkvin": wkvin,
        "wqp": wqp,
        "w_kvb": np.ascontiguousarray(np.asarray(w_kv_b, f32)[0]),
        "w_co": np.ascontiguousarray(np.asarray(w_conv_out, f32)[0]),
        "w_mo": np.ascontiguousarray(np.asarray(w_mla_out, f32)[0]),
        "w_o": np.ascontiguousarray(np.asarray(w_o, f32)[0]),
        "w_up": np.ascontiguousarray(np.asarray(w_ffn_up, f32)[0]),
        "w_dn": np.ascontiguousarray(np.asarray(w_ffn_down, f32)[0]),
    }
    cst = np.zeros((128, NC_CONST), f32)

    def put(name, arr):
        arr = np.asarray(arr, f32)
        cst[0:arr.shape[0], CO[name]:CO[name] + arr.shape[1]] = arr

    put("g1", _chunked(np.asarray(ln1_g, f32)[0], 16))
    put("bgate", _chunked(np.asarray(b_gate, f32)[0], 32))
    cw = np.asarray(conv_w, f32)[0]
    put("convw", np.stack([_chunked(cw[k], 8) for k in range(3)], axis=2).reshape(128, 24))
    put("qag", _chunked(np.asarray(q_a_g, f32)[0], 6))
    put("kvag", _chunked(np.asarray(kv_a_g, f32)[0], 4))
    gq = np.asarray(q_norm_g, f32)[0]
    gk = np.asarray(k_norm_g, f32)[0]
    put("gqn", gq[0:128, None])
    put("gqr", gq[128:192, None])
    put("gqrs", np.concatenate([gq[160:192], gq[128:160]])[:, None])
    put("gkn", gk[0:128, None])
    put("gkr", gk[128:192, None])
    put("gkrs", np.concatenate([gk[160:192], gk[128:160]])[:, None])
    put("g2", _chunked(np.asarray(ln2_g, f32)[0], 16))
    fw_ = np.asarray(ffn_conv_w, f32)[0]
    put("fcw", np.stack([_chunked(fw_[k], 88) for k in range(3)], axis=2).reshape(128, 264))
    put("fcb", _chunked(np.asarray(ffn_conv_b, f32)[0], 88))
    inv64 = 10000.0 ** (-np.arange(0, 64, 2, dtype=np.float64) / 64.0)
    inv_freq = inv64.astype(f32)
    inv_lo = (inv64 - inv_freq.astype(np.float64)).astype(f32)
    put("invf", np.concatenate([inv_freq, inv_freq])[:, None])
    put("invflo", np.concatenate([inv_lo, inv_lo])[:, None])
    put("sgn", np.concatenate([-np.ones(32, f32), np.ones(32, f32)])[:, None])
    pp = np.arange(128)[:, None]
    cc = np.arange(128)[None, :]
    put("tri", (pp <= cc).astype(f32))
    put("triH", (pp <= 124 + np.arange(4)[None, :]).astype(f32))

    in_maps = []
    for c in range(8):
        b, j = divmod(c, 4)
        start = j * OWN
        base = start - CTX
        xT = np.zeros((D, NKEY), f32)
        pos = np.zeros((NKEY,), np.int32)
        lo = max(0, -base)
        xT[:, lo:] = x[b, base + lo:start + OWN, :].T
        pos[lo:] = positions[b, base + lo:start + OWN]
        cst_c = cst.copy()
        kb = np.zeros((32,), f32)
        kb[: lo // 128] = NEG
        cst_c[:, CO["kbias"]:CO["kbias"] + 32] = kb[None, :]
        m = dict(shared)
        m["xT"] = xT
        m["posb"] = np.ascontiguousarray(np.broadcast_to(pos[None, :], (64, NKEY)))
        m["cst"] = cst_c
        in_maps.append(m)
    return in_maps


_PROG = {}


def kernel(**inputs):
    in_maps = prepare_inputs(**inputs)
    if "nc" not in _PROG:
        _PROG["nc"] = build_program()[0]
    nc = _PROG["nc"]
    res = run_bass_kernel_spmd(nc, in_maps, core_ids=list(range(8)))
    out = np.empty((2, 4096, D), np.float32)
    for c in range(8):
        b, j = divmod(c, 4)
        out[b, j * OWN:(j + 1) * OWN, :] = res.results[c]["outT"].T
    return out
```

```python
import math
import numpy as np
import concourse.bass as bass
import concourse.mybir as mybir
from concourse.bass_utils import run_bass_kernel_spmd

F32 = mybir.dt.float32
BF16 = mybir.dt.bfloat16
I32 = mybir.dt.int32
U8 = mybir.dt.uint8
AF = mybir.ActivationFunctionType
ALU = mybir.AluOpType

PE, ACT, DVE, POOL, SP = "tensor", "scalar", "vector", "gpsimd", "sync"
STRICT_SAME_ENGINE = True
ENGS = (PE, ACT, DVE, POOL, SP)


class T:
    __slots__ = ("name", "w", "r", "excl")

    def __init__(self, name="", excl=False):
        self.name = name
        self.w = None
        self.r = []
        self.excl = excl


class Op:
    __slots__ = ("eng", "fn", "deps", "is_dma", "chan", "done_val", "needs_inc", "waits")

    def __init__(self, eng, fn, is_dma, chan):
        self.eng = eng
        self.fn = fn
        self.is_dma = is_dma
        self.chan = chan
        self.deps = []
        self.done_val = None
        self.needs_inc = False
        self.waits = None


class Sched:
    def __init__(self, nc):
        self.nc = nc
        self.ops = {e: [] for e in ENGS}
        self.chan_cnt = {}
        self.chan_sem = {}
        self.eng_sem = {}
        self.last = {e: None for e in ENGS}

    def add(self, eng, fn, reads=(), writes=(), chan=None, extra_deps=()):
        is_dma = chan is not None
        o = Op(eng, fn, is_dma, chan)
        deps = {}

        def consider(d, kind):
            if d is None or d is o:
                return
            if d.is_dma or is_dma or d.eng != eng:
                deps[id(d)] = d
            elif eng == PE:
                return
            elif kind == "RAW" or STRICT_SAME_ENGINE:
                deps[id(d)] = d

        for t in reads:
            consider(t.w, "RAW")
            if t.excl:
                for r in t.r:
                    if r.eng != eng:
                        deps[id(r)] = r
        for t in writes:
            consider(t.w, "WAW")
            for r in t.r:
                consider(r, "WAR")
        for d in extra_deps:
            if d is not None and d is not o:
                deps[id(d)] = d
        for t in reads:
            t.r.append(o)
        for t in writes:
            t.w = o
            t.r = []
        o.deps = list(deps.values())
        if is_dma:
            c = self.chan_cnt.get(chan, 0) + 1
            self.chan_cnt[chan] = c
            o.done_val = 16 * c
        self.ops[eng].append(o)
        if not is_dma:
            self.last[eng] = o
        return o

    def barrier(self, engines=(PE, ACT, DVE, SP)):
        lasts = {e: self.last[e] for e in (PE, ACT, DVE)}
        for e in engines:
            deps = [lasts[x] for x in lasts if x != e and lasts[x] is not None]
            self.add(e, lambda eng: eng.nop(), extra_deps=deps)
            if e in lasts:
                self.last[e] = lasts[e]

    def finalize(self):
        nc = self.nc
        pos = {}
        for e in ENGS:
            for i, o in enumerate(self.ops[e]):
                pos[id(o)] = i
        for e in ENGS:
            waited = {}
            for o in self.ops[e]:
                best = {}
                for d in o.deps:
                    key = ("c", d.chan) if d.is_dma else ("e", d.eng)
                    p = d.done_val if d.is_dma else pos[id(d)]
                    if waited.get(key, -1) >= p:
                        continue
                    if key not in best or best[key][0] < p:
                        best[key] = (p, d)
                o.deps = [d for (_, d) in best.values()]
                for key, (p, d) in best.items():
                    waited[key] = p
                    d.needs_inc = True
        for e in ENGS:
            self.eng_sem[e] = nc.alloc_semaphore("sem_" + e)
            c = 0
            for o in self.ops[e]:
                if not o.is_dma and o.needs_inc:
                    c += 1
                    o.done_val = c
        for ch in self.chan_cnt:
            self.chan_sem[ch] = nc.alloc_semaphore("dsem_" + str(ch))
        ninc = 0
        nwait = 0
        for e in ENGS:
            for o in self.ops[e]:
                ws = []
                for d in o.deps:
                    sem = self.chan_sem[d.chan] if d.is_dma else self.eng_sem[d.eng]
                    ws.append((sem, d.done_val))
                o.waits = ws
                nwait += len(ws)
                ninc += 1 if o.needs_inc else 0
        self.stats = (ninc, nwait)
        print('sched stats: incs=%d waits=%d ops=%s' % (ninc, nwait, {e: len(self.ops[e]) for e in ENGS}))

    def emit(self, final_waits=()):
        nc = self.nc
        self.finalize()
        sched = self

        def replay(e, eng):
            sem_e = sched.eng_sem[e]
            for o in sched.ops[e]:
                for (sem, v) in o.waits:
                    eng.wait_ge(sem, v)
                ins = o.fn(eng)
                if o.is_dma:
                    ins.then_inc(sched.chan_sem[o.chan], 16)
                elif o.needs_inc:
                    ins.then_inc(sem_e, 1)

        with nc.Block() as block:
            @block.tensor
            def _(eng):
                replay(PE, eng)

            @block.scalar
            def _(eng):
                replay(ACT, eng)

            @block.vector
            def _(eng):
                replay(DVE, eng)

            @block.gpsimd
            def _(eng):
                replay(POOL, eng)

            @block.sync
            def _(eng):
                replay(SP, eng)
                for ch in final_waits:
                    eng.wait_ge(sched.chan_sem[ch], 16 * sched.chan_cnt[ch])


D = 2048
NKEY = 4096
CTX = 3072
OWN = 1024
HALO = 4
NE = OWN + HALO
TILES = [(0, HALO), (HALO, 512), (HALO + 512, 512)]
NH = 16
DFF = 5632
EPS = 1e-6
NEG = -30000.0
QSCALE = 192.0 ** -0.5
INTERLEAVE = True
PIPE = False

_c = {}
_o = 0
for _n, _w in [("g1", 16), ("bgate", 32), ("convw", 24), ("qag", 6), ("kvag", 4), ("gqn", 1), ("gqr", 1),
               ("gqrs", 1), ("gkn", 1), ("gkr", 1), ("gkrs", 1), ("g2", 16), ("fcw", 264), ("fcb", 88),
               ("invf", 1), ("invflo", 1), ("sgn", 1), ("kbias", 32), ("tri", 128), ("triH", 4)]:
    _c[_n] = _o
    _o += _w
CO = _c
NC_CONST = _o

KB = 1024
ARENA = 208896
A_CONST = 0
A_RING = 4 * KB
NSLOT = 4
A_UOWN = A_RING + NSLOT * 4 * KB
A_CT = A_UOWN + 32896
A_KROPE = A_CT + 32768
A_CQSQ = A_KROPE + 8192
A_SCR = A_CQSQ + 8256
A_ATT = A_SCR + 12352
A_ATTW = A_ATT + 32896


def build_program(debug=None):
    nc = bass.Bass("TRN2", target_bir_lowering=False)

    def din(name, shape, dt=F32):
        return nc.dram_tensor(name, list(shape), dt, kind="ExternalInput").ap()

    xT = din("xT", [D, NKEY])
    posb = din("posb", [64, NKEY], I32)
    cstd = din("cst", [128, NC_CONST])
    w_in = din("w_in", [D, 8512])
    wkvin = din("wkvin", [D, 640])
    wqp = din("wqp", [NH * 768, 256])
    w_kvb = din("w_kvb", [512, 4096])
    w_co = din("w_co", [1024, D])
    w_mo = din("w_mo", [D, D])
    w_o = din("w_o", [D, D])
    w_up = din("w_up", [D, 2 * DFF])
    w_dn = din("w_dn", [DFF, D])
    outT = nc.dram_tensor("outT", [D, OWN], F32, kind="ExternalOutput").ap()
    dbg = None
    dbg_items = []
    if debug is not None:
        dbg = nc.dram_tensor("dbg", [128, debug["n"]], F32, kind="ExternalOutput").ap()

    arena = nc.alloc_sbuf_tensor("arena", [128, ARENA], U8)

    def V(off, shape, dt):
        sz = 2 if dt == BF16 else 4
        nb = int(np.prod(shape)) * sz
        assert off % 4 == 0 and off + nb <= ARENA, (off, nb)
        v = arena[:, off:off + nb].bitcast(dt)
        if len(shape) == 2:
            v = v.rearrange("p (a b) -> p a b", b=shape[1])
        return v

    PW = [nc.alloc_psum_tensor("pw%d" % i, [128, 1024], F32) for i in range(2)]
    PS = [PW[0][:, 0:512], PW[0][:, 512:1024], PW[1][:, 0:512], PW[1][:, 512:1024]] + \
         [nc.alloc_psum_tensor("ps%d" % i, [128, 512], F32) for i in range(4, 8)]
    TP = [T("ps%d" % i, excl=True) for i in range(8)]

    S = Sched(nc)

    def mm(out, lhsT, rhs, start, stop, reads, writes):
        S.add(PE, lambda e: e.matmul(out, lhsT=lhsT, rhs=rhs, start=start, stop=stop), reads, writes)

    def act(out, in_, func, reads, writes, **kw):
        S.add(ACT, lambda e: e.activation(out=out, in_=in_, func=func, **kw), reads, writes)

    def stt(out, in0, scalar, in1, op0, op1, reads, writes):
        S.add(DVE, lambda e: e.scalar_tensor_tensor(out=out, in0=in0, scalar=scalar, in1=in1, op0=op0, op1=op1),
              reads, writes)

    def tt(out, in0, in1, op, reads, writes):
        S.add(DVE, lambda e: e.tensor_tensor(out=out, in0=in0, in1=in1, op=op), reads, writes)

    def ts(out, in0, s1, s2, op0, op1, reads, writes):
        if s2 is None:
            S.add(DVE, lambda e: e.tensor_scalar(out=out, in0=in0, scalar1=s1, scalar2=None, op0=op0), reads, writes)
        else:
            S.add(DVE, lambda e: e.tensor_scalar(out=out, in0=in0, scalar1=s1, scalar2=s2, op0=op0, op1=op1),
                  reads, writes)

    def cp(out, in_, reads, writes):
        S.add(DVE, lambda e: e.tensor_copy(out=out, in_=in_), reads, writes)

    def rsqrt_act(out, in_, n, reads, writes, tmp, ttmp, extra_bias=0.0):
        act(tmp, in_, AF.Ln, reads, [ttmp], scale=1.0 / n, bias=EPS)
        act(out, tmp, AF.Exp, [ttmp], writes, scale=-0.5, bias=extra_bias)

    cst = V(A_CONST, [NC_CONST], F32)
    Tcst = T("cst")
    ones = V(A_CONST + 2560, [128], BF16)
    tri = V(A_CONST + 2816, [128], BF16)
    triH = V(A_CONST + 3072, [4], BF16)
    krss = V(A_CONST + 3136, [32], F32)
    rkh = [V(A_CONST + 3264 + 128 * i, [32], F32) for i in range(2)]
    ksst = V(A_CONST + 3520, [32], F32)
    lnk = V(A_CONST + 3648, [32], F32)
    Tones, Ttri, Tkrss, Tksst, Tlnk = T("ones"), T("tri"), T("krss"), T("ksst"), T("lnk")
    Trkh = [T("rkh0"), T("rkh1")]
    gqe = V(A_CONST + 3776, [1], F32)
    Tgqe = T("gqe")

    def C(name, k=0, n=1, p=128):
        return cst[0:p, CO[name] + k:CO[name] + k + n]

    S.add(SP, lambda e: e.dma_start(out=cst, in_=cstd), writes=[Tcst], chan="cst")
    S.add(DVE, lambda e: e.memset(ones, 1.0), writes=[Tones])
    cp(tri, cst[:, CO["tri"]:CO["tri"] + 128], [Tcst], [Ttri])
    cp(triH, cst[:, CO["triH"]:CO["triH"] + 4], [Tcst], [Ttri])
    tt(gqe, cst[:, CO["gqn"]:CO["gqn"] + 1], cst[:, CO["gkn"]:CO["gkn"] + 1], ALU.mult, [Tcst], [Tgqe])

    ring = V(A_RING, [NSLOT, 2048], BF16)
    Tslot = [T("slot%d" % i) for i in range(NSLOT)]
    wblocks = []
    windex = {}

    def wdecl(key, src, r0, k, c0, n):
        windex[key] = len(wblocks)
        wblocks.append((key, src[r0:r0 + k * 128, c0:c0 + n].rearrange("(k p) n -> p k n", p=128), k, n))

    for i in range(6):
        wdecl(("qlat", i), w_in, 0, 16, 3072 + 128 * i, 128)
    for h in range(NH):
        wdecl(("hq", h), wqp, h * 768, 6, 0, 256)
        wdecl(("hkv", h), w_kvb, 0, 4, h * 256, 256)
    for i in range(8):
        wdecl(("zv", i), w_in, 0, 16, 2048 + 128 * i, 128)
        wdecl(("zc", i), w_in, 0, 16, 1024 + 128 * i, 128)
        wdecl(("zb", i), w_in, 0, 16, 128 * i, 128)
    for c in range(16):
        wdecl(("ga", c), w_in, 0, 16, 4416 + 128 * c, 128)
        wdecl(("gb", c), w_in, 0, 16, 6464 + 128 * c, 128)
        wdecl(("co", c), w_co, 0, 8, 128 * c, 128)
        wdecl(("mo", c), w_mo, 0, 16, 128 * c, 128)
    for c in range(16):
        wdecl(("wo", c), w_o, 0, 16, 128 * c, 128)
    for g in range(4):
        for jj in range(11):
            j = g * 11 + jj
            wdecl(("fg", j), w_up, 0, 16, 128 * j, 128)
            wdecl(("fu", j), w_up, 0, 16, DFF + 128 * j, 128)
        for c in range(16):
            wdecl(("fd", g, c), w_dn, g * 1408, 11, 128 * c, 128)

    wstate = {"next": 0, "done": 0}

    def wpump():
        while wstate["next"] < len(wblocks) and wstate["next"] - NSLOT < wstate["done"]:
            b = wstate["next"]
            key, src, k, n = wblocks[b]
            s = b % NSLOT
            dst = ring[:, s, 0:k * n].rearrange("p (k n) -> p k n", n=n)
            S.add(POOL, lambda e, dst=dst, src=src: e.dma_start(out=dst, in_=src), writes=[Tslot[s]], chan="w%d" % s)
            wstate["next"] += 1

    def wget(key):
        b = windex[key]
        assert b < wstate["done"] + NSLOT, key
        wpump()
        assert b < wstate["next"], key
        _, _, k, n = wblocks[b]
        s = b % NSLOT
        return ring[:, s, 0:k * n].rearrange("p (k n) -> p k n", n=n), Tslot[s]

    def wdone(key):
        b = windex[key]
        assert b == wstate["done"], (key, b, wstate["done"])
        wstate["done"] += 1
        wpump()

    dstate = {"off": 0}

    def dump(name, ap, shape, trackers):
        if debug is None or name not in debug["want"]:
            return
        n = int(np.prod(shape))
        o = dstate["off"]
        dst = dbg[0:ap.shape[0], o:o + n]
        if len(shape) == 2:
            dst = dst.rearrange("p (a b) -> p a b", b=shape[1])
        S.add(POOL, lambda e: e.dma_start(out=dst, in_=ap), reads=trackers, chan="dbg")
        dbg_items.append((name, o, n, tuple(shape), ap.shape[0]))
        dstate["off"] += n

    u_own = V(A_UOWN, [16, NE], BF16)
    Tu = {(k, t): T("u%d_%d" % (k, t)) for k in range(16) for t in range(3)}
    cT = V(A_CT, [4, NKEY], BF16)
    TcT = {(m, ci): T() for m in range(4) for ci in range(16)}
    kropeT = V(A_KROPE, [NKEY], BF16)
    Tkr = {ci: T() for ci in range(16)}
    S.add(DVE, lambda e: e.memset(kropeT[64:128, :], 0.0), writes=list(Tkr.values()))
    Cq = V(A_CQSQ, [NE], F32)
    Sq = V(A_CQSQ + 4128, [NE], F32)
    TCq = {t: T() for t in range(3)}

    NT1 = 256
    a = A_SCR
    xbuf = [V(a + i * 16384, [16, NT1], F32) for i in range(2)]
    a += 32768
    sq = V(a, [16, NT1], BF16)
    a += 8192
    uctx = [V(a + i * 8192, [16, NT1], BF16) for i in range(2)]
    a += 16384
    wkv_sb = V(a, [16, 640], BF16)
    a += 20480
    posi = [V(a + i * 1024, [NT1], I32) for i in range(2)]
    a += 2048
    rt = [V(a + i * 1024, [NT1], F32) for i in range(10)]
    a += 10240
    ni = V(a, [NT1], I32)
    a += 1024
    lnx = V(a, [NT1], F32)
    rstdx = V(a + 1024, [NT1], F32)
    lnkv = V(a + 2048, [NT1], F32)
    rstdkv = V(a + 3072, [NT1], F32)
    sqkv = V(a + 4096, [4, NT1], BF16)
    sqkr = V(a + 6144, [NT1], BF16)
    a += 6656
    assert a <= ARENA, a
    Txb = [[T() for q in range(4)] for i in range(2)]
    Tsq = [T() for q in range(4)]
    Tuc = [[T() for k in range(16)] for i in range(2)]
    Twkv = T("wkv")
    Tposi = [T(), T()]
    Trt = [T() for i in range(10)]
    Tni, Tlnx, Trstdx, Tlnkv, Trstdkv, Tsqkr = T(), T(), T(), T(), T(), T()
    Tsqkv = [T() for m in range(4)]

    S.add(POOL, lambda e: e.dma_start(out=wkv_sb, in_=wkvin.rearrange("(k p) n -> p k n", p=128)),
          writes=[Twkv], chan="wkv")
    wpump()

    INV2PI = 1.0 / (2.0 * math.pi)
    C1 = 6.28125
    C2 = 2.0 * math.pi - 6.28125
    PI_LO = 3.1415925

    def rope_tables(ci, pb, col0):
        posf, ang, tq, nf, r, Ct, St = rt[0], rt[1], rt[2], rt[3], rt[4], rt[5], rt[6]
        Tposf, Tang, Ttq, Tnf, Tr, TCt, TSt = Trt[0], Trt[1], Trt[2], Trt[3], Trt[4], Trt[5], Trt[6]
        p64 = slice(0, 64)
        cp(posf[p64], posi[pb][p64], [Tposi[pb]], [Tposf])
        ts(ang[p64], posf[p64], C("invf", p=64), None, ALU.mult, None, [Tposf, Tcst], [Tang])
        stt(ang[p64], posf[p64], C("invflo", p=64), ang[p64], ALU.mult, ALU.add, [Tposf, Tcst, Tang], [Tang])
        for which in (0, 1):
            if which == 0:
                ts(tq[p64], ang[p64], INV2PI, None, ALU.mult, None, [Tang], [Ttq])
            else:
                ts(tq[p64], ang[p64], INV2PI, 0.25, ALU.mult, ALU.add, [Tang], [Ttq])
            cp(ni[p64], tq[p64], [Ttq], [Tni])
            cp(nf[p64], ni[p64], [Tni], [Tnf])
            stt(r[p64], nf[p64], -C1, ang[p64], ALU.mult, ALU.add, [Tnf, Tang], [Tr])
            stt(r[p64], nf[p64], -C2, r[p64], ALU.mult, ALU.add, [Tnf, Tr], [Tr])
            if which == 1:
                ts(r[p64], r[p64], math.pi / 2.0, PI_LO, ALU.add, ALU.min, [Tr], [Tr])
            else:
                ts(r[p64], r[p64], PI_LO, None, ALU.min, None, [Tr], [Tr])
            ts(r[p64], r[p64], -PI_LO, None, ALU.max, None, [Tr], [Tr])
            if which == 0:
                act(St[p64], r[p64], AF.Sin, [Tr, Tcst], [TSt], scale=C("sgn", p=64))
            else:
                act(Ct[p64], r[p64], AF.Sin, [Tr], [TCt])
        return Ct, St, TCt, TSt

    def front(ci):
        col0 = ci * NT1
        b = ci % 2
        own = ci >= 12
        for q in range(4):
            src = xT[q * 512:(q + 1) * 512, col0:col0 + NT1].rearrange("(k p) n -> p k n", p=128)
            S.add(SP, lambda e, src=src, dst=xbuf[b][:, 4 * q:4 * q + 4, :]: e.dma_start(out=dst, in_=src),
                  writes=[Txb[b][q]], chan="x%d%d" % (b, q))
        S.add(SP, lambda e, dst=posi[b][0:64], src=posb[:, col0:col0 + NT1]: e.dma_start(out=dst, in_=src),
              writes=[Tposi[b]], chan="pos%d" % b)
        for q in range(4):
            act(sq[:, 4 * q:4 * q + 4, :], xbuf[b][:, 4 * q:4 * q + 4, :], AF.Square, [Txb[b][q]], [Tsq[q]])
        for k in range(16):
            mm(PS[0][:, 0:NT1], ones, sq[:, k, :], k == 0, k == 15, [Tones, Tsq[k // 4]], [TP[0]])
        rsqrt_act(rstdx, PS[0][:, 0:NT1], float(D), [TP[0]], [Trstdx], lnx, Tlnx)
        for k in range(16):
            if own:
                t = 1 + (ci - 12) // 2
                off = TILES[t][0] + ((ci - 12) % 2) * NT1
                dst, Tdst = u_own[:, k, off:off + NT1], Tu[(k, t)]
            else:
                dst, Tdst = uctx[b][:, k, :], Tuc[b][k]
            stt(dst, xbuf[b][:, k, :], C("g1", k), rstdx, ALU.mult, ALU.mult, [Txb[b][k // 4], Tcst, Trstdx], [Tdst])
        if ci == 11:
            cp(u_own[:, :, 0:HALO], uctx[b][:, :, NT1 - HALO:NT1], Tuc[b], [Tu[(k, 0)] for k in range(16)])

    def back(ci):
        col0 = ci * NT1
        b = ci % 2
        own = ci >= 12

        def usrc(k):
            if own:
                t = 1 + (ci - 12) // 2
                off = TILES[t][0] + ((ci - 12) % 2) * NT1
                return u_own[:, k, off:off + NT1], Tu[(k, t)]
            return uctx[b][:, k, :], Tuc[b][k]

        for m in range(4):
            for k in range(16):
                ua, Tua = usrc(k)
                mm(PS[1 + m][:, 0:NT1], wkv_sb[:, k, m * 128:(m + 1) * 128], ua, k == 0, k == 15, [Twkv, Tua], [TP[1 + m]])
        for k in range(16):
            ua, Tua = usrc(k)
            mm(PS[5][0:64, 0:NT1], wkv_sb[:, k, 512:576], ua, k == 0, k == 15, [Twkv, Tua], [TP[5]])
        for k in range(16):
            ua, Tua = usrc(k)
            mm(PS[6][0:64, 0:NT1], wkv_sb[:, k, 576:640], ua, k == 0, k == 15, [Twkv, Tua], [TP[6]])
        for m in range(4):
            act(sqkv[:, m, :], PS[1 + m][:, 0:NT1], AF.Square, [TP[1 + m]], [Tsqkv[m]])
        for m in range(4):
            mm(PS[7][:, 0:NT1], ones, sqkv[:, m, :], m == 0, m == 3, [Tones, Tsqkv[m]], [TP[7]])
        rsqrt_act(rstdkv, PS[7][:, 0:NT1], 512.0, [TP[7]], [Trstdkv], lnkv, Tlnkv)
        for m in range(4):
            stt(cT[:, m, col0:col0 + NT1], PS[1 + m][:, 0:NT1], C("kvag", m), rstdkv, ALU.mult, ALU.mult,
                [TP[1 + m], Tcst, Trstdkv], [TcT[(m, ci)]])
        Ct, St, TCt, TSt = rope_tables(ci, b, col0)
        p64 = slice(0, 64)
        if own or ci == 11:
            if own:
                t = 1 + (ci - 12) // 2
                off = TILES[t][0] + ((ci - 12) % 2) * NT1
                srcs = slice(0, NT1)
                n = NT1
            else:
                t, off, srcs, n = 0, 0, slice(NT1 - HALO, NT1), HALO
            ts(Cq[p64, off:off + n], Ct[p64, srcs], C("gqr", p=64), None, ALU.mult, None, [TCt, Tcst], [TCq[t]])
            ts(Sq[p64, off:off + n], St[p64, srcs], C("gqrs", p=64), None, ALU.mult, None, [TSt, Tcst], [TCq[t]])
        Ck, Sk, t1, t2 = rt[7], rt[8], rt[0], rt[2]
        TCk, TSk, Tt1, Tt2 = Trt[7], Trt[8], Trt[0], Trt[2]
        ts(Ck[p64], Ct[p64], C("gkr", p=64), None, ALU.mult, None, [TCt, Tcst], [TCk])
        ts(Sk[p64], St[p64], C("gkrs", p=64), None, ALU.mult, None, [TSt, Tcst], [TSk])
        tt(t1[p64], PS[5][0:64, 0:NT1], Ck[p64], ALU.mult, [TP[5], TCk], [Tt1])
        tt(t2[p64], PS[6][0:64, 0:NT1], Sk[p64], ALU.mult, [TP[6], TSk], [Tt2])
        tt(kropeT[0:64, col0:col0 + NT1], t1[p64], t2[p64], ALU.add, [Tt1, Tt2], [Tkr[ci]])
        act(sqkr[0:64], PS[5][0:64, 0:NT1], AF.Square, [TP[5]], [Tsqkr])
        for bb in range(2):
            mm(PS[7][:, 256 + bb:257 + bb], sqkr[0:64, bb * 128:(bb + 1) * 128], ones[0:64, 0:1], True, True,
               [Tsqkr, Tones], [TP[7]])
        cp(krss[:, 2 * ci:2 * ci + 2], PS[7][:, 256:258], [TP[7]], [Tkrss])

    front(0)
    for ci in range(16):
        if ci + 1 < 16:
            front(ci + 1)
        back(ci)

    dump("u_own", u_own, [16, NE], list(Tu.values()))
    dump("cT", cT, [4, NKEY], list(TcT.values()))
    dump("kropeT", kropeT[0:64], [NKEY], list(Tkr.values()))
    dump("krss", krss, [32], [Tkrss])
    dump("Cq", Cq[0:64], [NE], list(TCq.values()))
    dump("Sq", Sq[0:64], [NE], list(TCq.values()))
    S.barrier()
    if debug is not None and debug.get("stop") == "1a":
        return finish(nc, S, outT, dbg_items, debug)

    qn = V(A_SCR, [6, NE], BF16)
    Tqn = {(i, t): T() for i in range(6) for t in range(3)}
    a = A_SCR + 12352
    qlatf = V(a, [6, NE], F32)
    a += 24704
    sqq = V(a, [6, NE], BF16)
    a += 12352
    lnq = V(a, [NE], F32)
    a += 4160
    rstdq = V(a, [NE], F32)
    a += 4160
    Tqlf = {(i, t): T() for i in range(6) for t in range(3)}
    Tsqq = {(i, t): T() for i in range(6) for t in range(3)}
    Tlnq = [T() for t in range(3)]
    Trq = [T() for t in range(3)]
    pi = 0
    for i in range(6):
        wv, Tw = wget(("qlat", i))
        for t, (off, n) in enumerate(TILES):
            p = pi % 6
            pi += 1
            for k in range(16):
                mm(PS[p][:, 0:n], wv[:, k, :], u_own[:, k, off:off + n], k == 0, k == 15, [Tw, Tu[(k, t)]], [TP[p]])
            act(qlatf[:, i, off:off + n], PS[p][:, 0:n], AF.Copy, [TP[p]], [Tqlf[(i, t)]])
            act(sqq[:, i, off:off + n], PS[p][:, 0:n], AF.Square, [TP[p]], [Tsqq[(i, t)]])
        wdone(("qlat", i))
    for t, (off, n) in enumerate(TILES):
        p = 6 + (t % 2)
        for i in range(6):
            mm(PS[p][:, 0:n], ones, sqq[:, i, off:off + n], i == 0, i == 5, [Tones, Tsqq[(i, t)]], [TP[p]])
        rsqrt_act(rstdq[:, off:off + n], PS[p][:, 0:n], 768.0, [TP[p]], [Trq[t]], lnq[:, off:off + n], Tlnq[t])
        for i in range(6):
            stt(qn[:, i, off:off + n], qlatf[:, i, off:off + n], C("qag", i), rstdq[:, off:off + n], ALU.mult, ALU.mult,
                [Tqlf[(i, t)], Tcst, Trq[t]], [Tqn[(i, t)]])
    dump("qn", qn, [6, NE], list(Tqn.values()))
    S.barrier()
    if debug is not None and debug.get("stop") == "1b":
        return finish(nc, S, outT, dbg_items, debug)

    attnT = V(A_ATT, [16, NE], BF16)
    Tat = {(h, t): T() for h in range(NH) for t in range(3)}
    a = A_ATTW
    Kn = [V(a + i * 8192, [NKEY], BF16) for i in range(2)]
    a += 16384
    Vh = [V(a + i * 8192, [32, 128], BF16) for i in range(2)]
    a += 16384
    Qn = [V(a + i * 2112, [NE], BF16) for i in range(2)]
    a += 4224
    Qr = [V(a + i * 2112, [NE], BF16) for i in range(2)]
    a += 4224
    PT = [V(a + i * 2048, [1024], BF16) for i in range(3)]
    a += 6144
    sqk = [V(a + i * 1024, [512], BF16) for i in range(2)]
    a += 2048
    sqn = V(a, [512], BF16)
    a += 1024
    sqr = V(a, [512], BF16)
    a += 1024
    rstdqh = V(a, [256], F32)
    a += 1024
    qt1 = V(a, [256], F32)
    a += 1024
    qt2 = V(a, [256], F32)
    a += 1024
    acc = V(a, [1024], F32)
    a += 4096
    rl = acc
    accb = V(a, [1024], BF16)
    a += 2048
    assert a <= ARENA, a
    TKn = [[T() for ct in range(8)] for i in range(2)]
    TVh = [[T() for g in range(8)] for i in range(2)]
    TQn = [[T() for t in range(3)] for i in range(2)]
    TQr = [[T() for t in range(3)] for i in range(2)]
    TPT = [T() for i in range(3)]
    Tsqk = [T(), T()]
    Tsqn, Tsqr, Trstdqh, Tqt1, Tqt2, Tacc, Taccb = T(), T(), T(), T(), T(), T(), T()
    Trl = Tacc
    TcTcol = lambda m, c0, n: [TcT[(m, ci)] for ci in range(c0 // 256, (c0 + n + 255) // 256)]
    LNQS = math.log(QSCALE)
    for i_ in range(2):
        S.add(DVE, lambda e, i_=i_: e.memset(Qr[i_][64:128, :], 0.0), writes=TQr[i_])
    pti = 0
    sti = 0

    def prologue_units(h):
        hb = h % 2
        st = {}
        units = []

        def u_start():
            st["wq"], st["Twq"] = wget(("hq", h))
            st["wkv"], st["Twkv"] = wget(("hkv", h))

        def KU_stat(ct):
            sb = ct % 2
            for bb in range(4):
                mm(PS[6][:, 4 * ct + bb:4 * ct + bb + 1], sqk[sb][:, bb * 128:(bb + 1) * 128], ones[:, 0:1], True, True,
                   [Tsqk[sb], Tones], [TP[6]])

        def KU(ct):
            wkv, Twkvh = st["wkv"], st["Twkv"]
            p = 4 + (ct % 2)
            c0 = ct * 512
            for k in range(4):
                mm(PS[p][:, :], wkv[:, k, 0:128], cT[:, k, c0:c0 + 512], k == 0, k == 3,
                   [Twkvh] + TcTcol(k, c0, 512), [TP[p]])
            sb = ct % 2
            cp(Kn[hb][:, c0:c0 + 512], PS[p][:, :], [TP[p]], [TKn[hb][ct]])
            act(sqk[sb], Kn[hb][:, c0:c0 + 512], AF.Square, [TKn[hb][ct]], [Tsqk[sb]])
            if ct > 0:
                KU_stat(ct - 1)

        def KF():
            KU_stat(7)
            tt(ksst, PS[6][:, 0:32], krss, ALU.add, [TP[6], Tkrss], [Tksst])
            rsqrt_act(rkh[hb], ksst, 192.0, [Tksst], [Trkh[hb]], lnk, Tlnk, extra_bias=LNQS)

        def VU(g):
            wkv, Twkvh = st["wkv"], st["Twkv"]
            p = 6 + (g % 2)
            for bb in range(4):
                kb = 4 * g + bb
                for k in range(4):
                    mm(PS[p][:, bb * 128:(bb + 1) * 128], cT[:, k, kb * 128:(kb + 1) * 128], wkv[:, k, 128:256],
                       k == 0, k == 3, [Twkvh] + TcTcol(k, kb * 128, 128), [TP[p]])
            cp(Vh[hb][:, 4 * g:4 * g + 4, :], PS[p][:, :].rearrange("p (a b) -> p a b", b=128), [TP[p]], [TVh[hb][g]])

        def QU1(t):
            wq, Twq = st["wq"], st["Twq"]
            off, n = TILES[t]
            for k in range(6):
                mm(PS[5][:, 0:n], wq[:, k, 0:128], qn[:, k, off:off + n], k == 0, k == 5, [Twq, Tqn[(k, t)]], [TP[5]])
            for k in range(6):
                mm(PS[6][0:64, 0:n], wq[:, k, 128:192], qn[:, k, off:off + n], k == 0, k == 5, [Twq, Tqn[(k, t)]], [TP[6]])
            for k in range(6):
                mm(PS[7][0:64, 0:n], wq[:, k, 192:256], qn[:, k, off:off + n], k == 0, k == 5, [Twq, Tqn[(k, t)]], [TP[7]])
            act(sqn[:, 0:n], PS[5][:, 0:n], AF.Square, [TP[5]], [Tsqn])
            act(sqr[0:64, 0:n], PS[6][0:64, 0:n], AF.Square, [TP[6]], [Tsqr])

        def QU2(t):
            off, n = TILES[t]
            mm(PS[4][:, 0:n], ones, sqn[:, 0:n], True, False, [Tones, Tsqn], [TP[4]])
            mm(PS[4][:, 0:n], ones[0:64, :], sqr[0:64, 0:n], False, True, [Tones, Tsqr], [TP[4]])
            for c0 in range(0, n, 256):
                m = min(256, n - c0)
                ps_, pe_ = c0, c0 + m
                rsqrt_act(rstdqh[:, 0:m], PS[4][:, ps_:pe_], 192.0, [TP[4]], [Trstdqh], rstdqh[:, 0:m], Trstdqh)
                stt(Qn[hb][:, off + ps_:off + pe_], PS[5][:, ps_:pe_], gqe, rstdqh[:, 0:m], ALU.mult, ALU.mult,
                    [TP[5], Tgqe, Trstdqh], [TQn[hb][t]])
                tt(qt1[0:64, 0:m], PS[6][0:64, ps_:pe_], Cq[0:64, off + ps_:off + pe_], ALU.mult, [TP[6], TCq[t]], [Tqt1])
                tt(qt2[0:64, 0:m], PS[7][0:64, ps_:pe_], Sq[0:64, off + ps_:off + pe_], ALU.mult, [TP[7], TCq[t]], [Tqt2])
                tt(qt1[0:64, 0:m], qt1[0:64, 0:m], qt2[0:64, 0:m], ALU.add, [Tqt1, Tqt2], [Tqt1])
                tt(Qr[hb][0:64, off + ps_:off + pe_], qt1[0:64, 0:m], rstdqh[0:64, 0:m], ALU.mult, [Tqt1, Trstdqh],
                   [TQr[hb][t]])

        def u_end():
            wdone(("hq", h))
            wdone(("hkv", h))

        units.append(u_start)
        for ct in range(8):
            units.append(lambda ct=ct: KU(ct))
        units.append(KF)
        for t in range(3):
            units.append(lambda t=t: QU1(t))
            units.append(lambda t=t: QU2(t))
        vunits = [(lambda g=g: VU(g)) for g in range(8)]
        return units, vunits, u_end

    hsteps = [(0, 0, HALO, kb, 24, kb - 23, 0) for kb in range(24)]
    SBANK = [0, 1, 2]

    def emit_qk(h, step):
        nonlocal sti, pti
        hb = h % 2
        t, off, n, kb, nblk, d, lo = step
        sp = SBANK[sti % 3]
        sti += 1
        pt = pti % 3
        pti += 1
        mm(PS[sp][:, 0:n], Kn[hb][:, kb * 128:(kb + 1) * 128], Qn[hb][:, off:off + n], True, False,
           [TKn[hb][kb // 4], TQn[hb][t]], [TP[sp]])
        mm(PS[sp][:, 0:n], kropeT[:, kb * 128:(kb + 1) * 128], Qr[hb][:, off:off + n], False, True,
           [Tkr[kb // 2], TQr[hb][t]], [TP[sp]])
        act(PT[pt][:, 0:n], PS[sp][:, 0:n], AF.Exp, [TP[sp], Trkh[hb], Tcst], [TPT[pt]],
            scale=rkh[hb][:, kb:kb + 1], bias=C("kbias", kb))
        if d >= 0:
            tt(PT[pt][:, 0:HALO], PT[pt][:, 0:HALO], triH, ALU.mult, [TPT[pt], Ttri], [TPT[pt]])
        return pt

    def emit_pv(h, step, pt):
        nonlocal sti
        hb = h % 2
        t, off, n, kb, nblk, d, lo = step
        po = 3
        mm(PS[po][:, 0:n], Vh[hb][:, kb, :], PT[pt][:, 0:n], kb == 0, kb == nblk - 1, [TVh[hb][kb // 4], TPT[pt]], [TP[po]])
        if kb == 0:
            cp(acc[:, 0:n], PT[pt][:, 0:n], [TPT[pt]], [Tacc])
        else:
            tt(acc[:, 0:n], acc[:, 0:n], PT[pt][:, 0:n], ALU.add, [Tacc, TPT[pt]], [Tacc])
        if kb == nblk - 1:
            cp(accb[:, 0:n], acc[:, 0:n], [Tacc], [Taccb])
            sp = SBANK[sti % 3]
            sti += 1
            mm(PS[sp][:, 0:n], ones, accb[:, 0:n], True, True, [Tones, Taccb], [TP[sp]])
            act(rl[:, 0:n], PS[sp][:, 0:n], AF.Ln, [TP[sp]], [Trl], bias=1e-30)
            act(rl[:, 0:n], rl[:, 0:n], AF.Exp, [Trl], [Trl], scale=-1.0)
            tt(attnT[:, h, off:off + n], PS[po][:, 0:n], rl[:, 0:n], ALU.mult, [TP[po], Trl], [Tat[(h, t)]])

    C0 = HALO
    dsteps = []
    for kb in range(32):
        if kb < 24:
            dsteps.append((kb, 0, 0))
        elif kb < 28:
            dsteps.append((kb, 128 * (kb - 24), 128 * (kb - 24)))
        else:
            dsteps.append((kb, 128 * (kb - 28), 512 + 128 * (kb - 28)))
    dcnt = [0]

    def emit_qk2(h, dstep):
        hb = h % 2
        kb, lo, c_lo = dstep
        x = dcnt[0] % 2
        y = dcnt[0] % 3
        dcnt[0] += 1
        pw = PW[x]
        Tpw = [TP[2 * x], TP[2 * x + 1]]
        kblk = slice(kb * 128, (kb + 1) * 128)
        if kb < 28:
            mm(pw[:, c_lo:512], Kn[hb][:, kblk], Qn[hb][:, C0 + c_lo:C0 + 512], True, False,
               [TKn[hb][kb // 4], TQn[hb][1]], [Tpw[0]])
            mm(pw[:, c_lo:512], kropeT[:, kblk], Qr[hb][:, C0 + c_lo:C0 + 512], False, True,
               [Tkr[kb // 2], TQr[hb][1]], [Tpw[0]])
        c1 = max(c_lo, 512)
        mm(pw[:, c1:1024], Kn[hb][:, kblk], Qn[hb][:, C0 + c1:C0 + 1024], True, False,
           [TKn[hb][kb // 4], TQn[hb][2]], [Tpw[1]])
        mm(pw[:, c1:1024], kropeT[:, kblk], Qr[hb][:, C0 + c1:C0 + 1024], False, True,
           [Tkr[kb // 2], TQr[hb][2]], [Tpw[1]])
        rd = Tpw if kb < 28 else [Tpw[1]]
        act(PT[y][:, c_lo:1024], pw[:, c_lo:1024], AF.Exp, rd + [Trkh[hb], Tcst], [TPT[y]],
            scale=rkh[hb][:, kb:kb + 1], bias=C("kbias", kb))
        if kb >= 24:
            tt(PT[y][:, c_lo:c_lo + 128], PT[y][:, c_lo:c_lo + 128], tri, ALU.mult, [TPT[y], Ttri], [TPT[y]])
        return y

    def emit_pv2(h, dstep, x):
        hb = h % 2
        kb, lo, c_lo = dstep
        if kb < 28:
            mm(PS[4][:, c_lo:512], Vh[hb][:, kb, :], PT[x][:, c_lo:512], kb == 0, kb == 27,
               [TVh[hb][kb // 4], TPT[x]], [TP[4]])
        c1 = max(c_lo, 512)
        mm(PS[5][:, c1 - 512:512], Vh[hb][:, kb, :], PT[x][:, c1:1024], kb == 0, kb == 31,
           [TVh[hb][kb // 4], TPT[x]], [TP[5]])
        if kb == 0:
            cp(acc[:, 0:1024], PT[x][:, 0:1024], [TPT[x]], [Tacc])
        else:
            tt(acc[:, c_lo:1024], acc[:, c_lo:1024], PT[x][:, c_lo:1024], ALU.add, [Tacc, TPT[x]], [Tacc])
        if kb == 31:
            cp(accb[:, 0:1024], acc[:, 0:1024], [Tacc], [Taccb])
            for half in range(2):
                hs = slice(512 * half, 512 * half + 512)
                mm(PS[6 + half][:, :], ones, accb[:, hs], True, True, [Tones, Taccb], [TP[6 + half]])
                act(rl[:, hs], PS[6 + half][:, :], AF.Ln, [TP[6 + half]], [Trl], bias=1e-30)
            act(rl[:, 0:1024], rl[:, 0:1024], AF.Exp, [Trl], [Trl], scale=-1.0)
            for half in range(2):
                hs = slice(512 * half, 512 * half + 512)
                tt(attnT[:, h, C0 + 512 * half:C0 + 512 * half + 512], PS[4 + half][:, :], rl[:, hs], ALU.mult,
                   [TP[4 + half], Trl], [Tat[(h, 1 + half)]])

    u0, v0, e0 = prologue_units(0)
    for u in u0 + v0 + [e0]:
        u()
    for h in range(NH):
        if h + 1 < NH:
            units, vunits, u_end = prologue_units(h + 1)
        else:
            units, vunits, u_end = [], [], None
        ns = len(hsteps)
        pts = [None] * ns
        pts[0] = emit_qk(h, hsteps[0])
        pts[1] = emit_qk(h, hsteps[1])
        for i in range(ns):
            if i + 2 < ns:
                pts[i + 2] = emit_qk(h, hsteps[i + 2])
            emit_pv(h, hsteps[i], pts[i])
            if units and i % 3 != 2:
                units.pop(0)()
        while units:
            units.pop(0)()
        nd = len(dsteps)
        xs = [None] * nd
        xs[0] = emit_qk2(h, dsteps[0])
        for i in range(nd):
            if i + 1 < nd:
                xs[i + 1] = emit_qk2(h, dsteps[i + 1])
            if vunits and i % 4 == 1 and i < 30:
                vunits.pop(0)()
            emit_pv2(h, dsteps[i], xs[i])
        while vunits:
            vunits.pop(0)()
        if u_end is not None:
            u_end()
        if h == 0:
            dump("Kn0", Kn[0], [NKEY], TKn[0])
            dump("Vh0", Vh[0], [32, 128], TVh[0])
            dump("Qn0", Qn[0], [NE], TQn[0])
            dump("Qr0", Qr[0][0:64], [NE], TQr[0])
            dump("rk0", rkh[0], [32], [Trkh[0]])
    dump("attnT", attnT, [16, NE], list(Tat.values()))
    S.barrier()
    if debug is not None and debug.get("stop") == "2":
        return finish(nc, S, outT, dbg_items, debug)

    gconvT = V(A_CT, [8, NE], BF16)
    Tgc = {i: T() for i in range(8)}
    a = A_CT + 16448
    cv = V(a, [NE], F32)
    zvs = V(a + 4160, [NE], F32)
    zbs = V(a + 8320, [NE], F32)
    yc = V(a + 12480, [NE], F32)
    A_S4 = a + 16640
    Tcv, Tzvs, Tzbs, Tyc = T(), T(), T(), T()
    S.add(DVE, lambda e: e.memset(gconvT[:, :, 0:2], 0.0), writes=list(Tgc.values()))
    for i in range(8):
        pi3 = 0
        for (key, kind) in ((("zv", i), "v"), (("zc", i), "c"), (("zb", i), "b")):
            wx, Twx = wget(key)
            for t, (off, n) in enumerate(TILES):
                p = pi3 % 6
                pi3 += 1
                for k in range(16):
                    mm(PS[p][:, 0:n], wx[:, k, :], u_own[:, k, off:off + n], k == 0, k == 15, [Twx, Tu[(k, t)]], [TP[p]])
                if kind == "v":
                    act(zvs[:, off:off + n], PS[p][:, 0:n], AF.Copy, [TP[p]], [Tzvs])
                elif kind == "c":
                    tt(cv[:, off:off + n], PS[p][:, 0:n], zvs[:, off:off + n], ALU.mult, [TP[p], Tzvs], [Tcv])
                else:
                    act(zbs[:, off:off + n], PS[p][:, 0:n], AF.Copy, [TP[p]], [Tzbs])
            wdone(key)
        m = NE - 2
        ts(yc[:, 2:NE], cv[:, 2:NE], C("convw", 3 * i + 2), None, ALU.mult, None, [Tcv, Tcst], [Tyc])
        stt(yc[:, 2:NE], cv[:, 1:NE - 1], C("convw", 3 * i + 1), yc[:, 2:NE], ALU.mult, ALU.add, [Tcv, Tcst, Tyc], [Tyc])
        stt(yc[:, 2:NE], cv[:, 0:NE - 2], C("convw", 3 * i + 0), yc[:, 2:NE], ALU.mult, ALU.add, [Tcv, Tcst, Tyc], [Tyc])
        tt(gconvT[:, i, 2:NE], yc[:, 2:NE], zbs[:, 2:NE], ALU.mult, [Tyc, Tzbs], [Tgc[i]])
    dump("gconvT", gconvT, [8, NE], list(Tgc.values()))

    mT = V(A_ATTW, [16, NE], BF16)
    Tm = {(c, t): T() for c in range(16) for t in range(3)}
    a = A_S4
    sga = V(a, [NE], F32)
    sgb = V(a + 4160, [NE], F32)
    mt1 = V(a + 8320, [512], F32)
    mt2 = V(a + 8320 + 2048, [512], F32)
    assert a + 8320 + 4096 <= A_ATT
    Tsga = [T() for t in range(3)]
    Tsgb = [T() for t in range(3)]
    Tmt1, Tmt2 = T(), T()
    for c in range(16):
        wga, Twga = wget(("ga", c))
        wgb, Twgb = wget(("gb", c))
        for t, (off, n) in enumerate(TILES):
            pa, pb_ = (0, 1) if t % 2 == 0 else (2, 3)
            for k in range(16):
                mm(PS[pa][:, 0:n], wga[:, k, :], u_own[:, k, off:off + n], k == 0, k == 15, [Twga, Tu[(k, t)]], [TP[pa]])
            for k in range(16):
                mm(PS[pb_][:, 0:n], wgb[:, k, :], u_own[:, k, off:off + n], k == 0, k == 15, [Twgb, Tu[(k, t)]], [TP[pb_]])
            act(sga[:, off:off + n], PS[pa][:, 0:n], AF.Sigmoid, [TP[pa], Tcst], [Tsga[t]], bias=C("bgate", c))
            act(sgb[:, off:off + n], PS[pb_][:, 0:n], AF.Sigmoid, [TP[pb_], Tcst], [Tsgb[t]], bias=C("bgate", 16 + c))
        wdone(("ga", c))
        wdone(("gb", c))
        wco, Twco = wget(("co", c))
        wmo, Twmo = wget(("mo", c))
        for t, (off, n) in enumerate(TILES):
            pa, pb_ = (4, 5) if t % 2 == 0 else (6, 7)
            for k in range(8):
                mm(PS[pa][:, 0:n], wco[:, k, :], gconvT[:, k, off:off + n], k == 0, k == 7, [Twco, Tgc[k]], [TP[pa]])
            for k in range(16):
                mm(PS[pb_][:, 0:n], wmo[:, k, :], attnT[:, k, off:off + n], k == 0, k == 15, [Twmo, Tat[(k, t)]], [TP[pb_]])
            tt(mt1[:, 0:n], PS[pa][:, 0:n], sga[:, off:off + n], ALU.mult, [TP[pa], Tsga[t]], [Tmt1])
            tt(mt2[:, 0:n], PS[pb_][:, 0:n], sgb[:, off:off + n], ALU.mult, [TP[pb_], Tsgb[t]], [Tmt2])
            tt(mT[:, c, off:off + n], mt1[:, 0:n], mt2[:, 0:n], ALU.add, [Tmt1, Tmt2], [Tm[(c, t)]])
        wdone(("co", c))
        wdone(("mo", c))
    dump("mT", mT, [16, NE], list(Tm.values()))
    S.barrier()

    hres = V(A_UOWN, [16, NE], F32)
    Th = {(c, t): T() for c in range(16) for t in range(3)}
    a = A_UOWN + 65792
    sq2 = V(a, [16, NE], BF16)
    Tsq2 = {(c, t): T() for c in range(16) for t in range(3)}
    u2 = V(a, [16, NE], BF16)
    a += 32896
    ln2 = V(a, [NE], F32)
    rstd2 = V(a + 4160, [NE], F32)
    a += 8320
    A_S7 = a
    assert a <= A_ATTW
    Tln2 = [T() for t in range(3)]
    Trstd2 = [T() for t in range(3)]
    for c in range(16):
        src = xT[c * 128:(c + 1) * 128, CTX - HALO:NKEY]
        Tc3 = [Th[(c, t)] for t in range(3)]
        S.add(SP, lambda e, src=src, dst=hres[:, c, :]: e.dma_start(out=dst, in_=src), writes=Tc3, chan="xr%d" % c)
    for c in range(16):
        wo, Two = wget(("wo", c))
        for t, (off, n) in enumerate(TILES):
            p = (3 * c + t) % 5
            for k in range(16):
                mm(PS[p][:, 0:n], wo[:, k, :], mT[:, k, off:off + n], k == 0, k == 15, [Two, Tm[(k, t)]], [TP[p]])
            tt(hres[:, c, off:off + n], PS[p][:, 0:n], hres[:, c, off:off + n], ALU.add, [TP[p], Th[(c, t)]], [Th[(c, t)]])
            act(sq2[:, c, off:off + n], hres[:, c, off:off + n], AF.Square, [Th[(c, t)]], [Tsq2[(c, t)]])
        wdone(("wo", c))
    for t, (off, n) in enumerate(TILES):
        p = 5 + t
        for c in range(16):
            mm(PS[p][:, 0:n], ones, sq2[:, c, off:off + n], c == 0, c == 15, [Tones, Tsq2[(c, t)]], [TP[p]])
        rsqrt_act(rstd2[:, off:off + n], PS[p][:, 0:n], float(D), [TP[p]], [Trstd2[t]], ln2[:, off:off + n], Tln2[t])
    dump("h1", hres, [16, NE], list(Th.values()))
    S.barrier()

    Tu2 = {(k, t): T() for k in range(16) for t in range(3)}
    for k in range(16):
        for t, (off, n) in enumerate(TILES):
            stt(u2[:, k, off:off + n], hres[:, k, off:off + n], C("g2", k), rstd2[:, off:off + n], ALU.mult, ALU.mult,
                [Th[(k, t)], Tcst, Trstd2[t]], [Tu2[(k, t)]])
    dump("u2", u2, [16, NE], list(Tu2.values()))

    a = A_S7
    Ag = [V(a + i * 4160, [NE], F32) for i in range(2)]
    a += 8320
    Au = [V(a + i * 4160, [NE], F32) for i in range(2)]
    a += 8320
    yg = V(a, [OWN], F32)
    yu = V(a + 4096, [OWN], F32)
    a += 8192
    fT = [V(a + i * 22528, [11, OWN], BF16) for i in range(2)]
    a += 45056
    assert a <= ARENA, a
    TAg = [[T() for t in range(3)] for i in range(2)]
    TAu = [[T() for t in range(3)] for i in range(2)]
    Tyg, Tyu = T(), T()
    TfT = [[T() for jj in range(11)] for i in range(2)]
    H0 = HALO
    for g in range(4):
        gb = g % 2
        for jj in range(11):
            j = g * 11 + jj
            ab = j % 2
            wg_, Twg = wget(("fg", j))
            wu_, Twu = wget(("fu", j))
            for t, (off, n) in ((1, TILES[1]), (2, TILES[2]), (0, TILES[0])):
                pg, pu = {1: (0, 1), 2: (2, 3), 0: (4, 5)}[t]
                for k in range(16):
                    mm(PS[pg][:, 0:n], wg_[:, k, :], u2[:, k, off:off + n], k == 0, k == 15, [Twg, Tu2[(k, t)]], [TP[pg]])
                for k in range(16):
                    mm(PS[pu][:, 0:n], wu_[:, k, :], u2[:, k, off:off + n], k == 0, k == 15, [Twu, Tu2[(k, t)]], [TP[pu]])
                act(Ag[ab][:, off:off + n], PS[pg][:, 0:n], AF.Copy, [TP[pg]], [TAg[ab][t]])
                act(Au[ab][:, off:off + n], PS[pu][:, 0:n], AF.Copy, [TP[pu]], [TAu[ab][t]])
            wdone(("fg", j))
            wdone(("fu", j))
            for (A_, TA_, y_, Ty_, jc) in ((Ag[ab], TAg[ab], yg, Tyg, j), (Au[ab], TAu[ab], yu, Tyu, 44 + j)):
                ts(y_, A_[:, H0:NE], C("fcw", 3 * jc + 2), C("fcb", jc), ALU.mult, ALU.add, TA_ + [Tcst], [Ty_])
                stt(y_, A_[:, H0 - 1:NE - 1], C("fcw", 3 * jc + 1), y_, ALU.mult, ALU.add, TA_ + [Tcst, Ty_], [Ty_])
                stt(y_, A_[:, H0 - 2:NE - 2], C("fcw", 3 * jc + 0), y_, ALU.mult, ALU.add, TA_ + [Tcst, Ty_], [Ty_])
            act(yg, yg, AF.Silu, [Tyg], [Tyg])
            tt(fT[gb][:, jj, :], yg, yu, ALU.mult, [Tyg, Tyu], [TfT[gb][jj]])
        for c in range(16):
            wd, Twd = wget(("fd", g, c))
            for t in (1, 2):
                off, n = TILES[t]
                p = 6 + (t % 2)
                for jj in range(11):
                    mm(PS[p][:, 0:n], wd[:, jj, :], fT[gb][:, jj, off - H0:off - H0 + n], jj == 0, jj == 10,
                       [Twd, TfT[gb][jj]], [TP[p]])
                tt(hres[:, c, off:off + n], PS[p][:, 0:n], hres[:, c, off:off + n], ALU.add, [TP[p], Th[(c, t)]], [Th[(c, t)]])
            wdone(("fd", g, c))
            if g == 3:
                S.add(SP, lambda e, dst=outT[c * 128:(c + 1) * 128, :], src=hres[:, c, H0:NE]: e.dma_start(out=dst, in_=src),
                      reads=[Th[(c, 1)], Th[(c, 2)]], chan="out%d" % c)
    return finish(nc, S, outT, dbg_items, debug, outs=True)


def finish(nc, S, outT, dbg_items, debug, outs=False):
    fw = []
    if outs:
        fw += ["out%d" % i for i in range(16)]
    if debug is not None and "dbg" in S.chan_cnt:
        fw.append("dbg")
    if debug is not None and not outs:
        pass
    S.emit(final_waits=fw)
    return nc, dbg_items


def _chunked(v, nchunk):
    return np.ascontiguousarray(v.reshape(nchunk, 128).T)


def prepare_inputs(x, positions, ln1_g, w_in, b_gate, conv_w, w_conv_out, q_a_g, w_q_b, kv_a_g, w_kv_b,
                   q_norm_g, k_norm_g, w_mla_out, w_o, ln2_g, w_ffn_up, ffn_conv_w, ffn_conv_b, w_ffn_down):
    f32 = np.float32
    x = np.asarray(x, f32)
    positions = np.asarray(positions)
    w_in0 = np.ascontiguousarray(np.asarray(w_in, f32)[0])
    wq0 = np.asarray(w_q_b, f32)[0]
    kr0 = 4352
    wkvin = np.ascontiguousarray(np.concatenate(
        [w_in0[:, 3840:4416], w_in0[:, kr0 + 32:kr0 + 64], w_in0[:, kr0:kr0 + 32]], axis=1))
    wqp = np.empty((NH, 768, 256), f32)
    for h in range(NH):
        b0 = h * 192
        wqp[h, :, 0:128] = wq0[:, b0:b0 + 128]
        wqp[h, :, 128:192] = wq0[:, b0 + 128:b0 + 192]
        wqp[h, :, 192:224] = wq0[:, b0 + 160:b0 + 192]
        wqp[h, :, 224:256] = wq0[:, b0 + 128:b0 + 160]
    wqp = wqp.reshape(NH * 768, 256)
    shared = {
        "w_in": w_in0,
        "wkvin": wkvin,
        "wqp": wqp,
        "w_kvb": np.ascontiguousarray(np.asarray(w_kv_b, f32)[0]),
        "w_co": np.ascontiguousarray(np.asarray(w_conv_out, f32)[0]),
        "w_mo": np.ascontiguousarray(np.asarray(w_mla_out, f32)[0]),
        "w_o": np.ascontiguousarray(np.asarray(w_o, f32)[0]),
        "w_up": np.ascontiguousarray(np.asarray(w_ffn_up, f32)[0]),
        "w_dn": np.ascontiguousarray(np.asarray(w_ffn_down, f32)[0]),
    }
    cst = np.zeros((128, NC_CONST), f32)

    def put(name, arr):
        arr = np.asarray(arr, f32)
        cst[0:arr.shape[0], CO[name]:CO[name] + arr.shape[1]] = arr

    put("g1", _chunked(np.asarray(ln1_g, f32)[0], 16))
    put("bgate", _chunked(np.asarray(b_gate, f32)[0], 32))
    cw = np.asarray(conv_w, f32)[0]
    put("convw", np.stack([_chunked(cw[k], 8) for k in range(3)], axis=2).reshape(128, 24))
    put("qag", _chunked(np.asarray(q_a_g, f32)[0], 6))
    put("kvag", _chunked(np.asarray(kv_a_g, f32)[0], 4))
    gq = np.asarray(q_norm_g, f32)[0]
    gk = np.asarray(k_norm_g, f32)[0]
    put("gqn", gq[0:128, None])
    put("gqr", gq[128:192, None])
    put("gqrs", np.concatenate([gq[160:192], gq[128:160]])[:, None])
    put("gkn", gk[0:128, None])
    put("gkr", gk[128:192, None])
    put("gkrs", np.concatenate([gk[160:192], gk[128:160]])[:, None])
    put("g2", _chunked(np.asarray(ln2_g, f32)[0], 16))
    fw_ = np.asarray(ffn_conv_w, f32)[0]
    put("fcw", np.stack([_chunked(fw_[k], 88) for k in range(3)], axis=2).reshape(128, 264))
    put("fcb", _chunked(np.asarray(ffn_conv_b, f32)[0], 88))
    inv64 = 10000.0 ** (-np.arange(0, 64, 2, dtype=np.float64) / 64.0)
    inv_freq = inv64.astype(f32)
    inv_lo = (inv64 - inv_freq.astype(np.float64)).astype(f32)
    put("invf", np.concatenate([inv_freq, inv_freq])[:, None])
    put("invflo", np.concatenate([inv_lo, inv_lo])[:, None])
    put("sgn", np.concatenate([-np.ones(32, f32), np.ones(32, f32)])[:, None])
    pp = np.arange(128)[:, None]
    cc = np.arange(128)[None, :]
    put("tri", (pp <= cc).astype(f32))
    put("triH", (pp <= 124 + np.arange(4)[None, :]).astype(f32))

    in_maps = []
    for c in range(8):
        b, j = divmod(c, 4)
        start = j * OWN
        base = start - CTX
        xT = np.zeros((D, NKEY), f32)
        pos = np.zeros((NKEY,), np.int32)
        lo = max(0, -base)
        xT[:, lo:] = x[b, base + lo:start + OWN, :].T
        pos[lo:] = positions[b, base + lo:start + OWN]
        cst_c = cst.copy()
        kb = np.zeros((32,), f32)
        kb[: lo // 128] = NEG
        cst_c[:, CO["kbias"]:CO["kbias"] + 32] = kb[None, :]
        m = dict(shared)
        m["xT"] = xT
        m["posb"] = np.ascontiguousarray(np.broadcast_to(pos[None, :], (64, NKEY)))
        m["cst"] = cst_c
        in_maps.append(m)
    return in_maps


_PROG = {}


def kernel(**inputs):
    in_maps = prepare_inputs(**inputs)
    if "nc" not in _PROG:
        _PROG["nc"] = build_program()[0]
    nc = _PROG["nc"]
    res = run_bass_kernel_spmd(nc, in_maps, core_ids=list(range(8)))
    out = np.empty((2, 4096, D), np.float32)
    for c in range(8):
        b, j = divmod(c, 4)
        out[b, j * OWN:(j + 1) * OWN, :] = res.results[c]["outT"].T
    return out
```
